# Optimizing a Trainium2 kernel written in Bass

```python
import jax, jax.numpy as jnp
from jax import lax
import numpy as np

D_MODEL = 1024
BATCH = 16
SEQ = 2048
DEPTH = 1

P_DIM = 256
LRU_WIDTH = 1024
LRU_HEADS = 8
LRU_HEAD_DIM = LRU_WIDTH // LRU_HEADS
CONV_WIDTH = 4
LRU_C = 8.0
POOL_WIDTH = D_MODEL // 2
POOL_WINDOWS = (2, 4, 8, 16)
POOL_GROUPS = len(POOL_WINDOWS)
POOL_GROUP_DIM = POOL_WIDTH // POOL_GROUPS
MAX_WIN = max(POOL_WINDOWS)
IN_COLS = 2 * LRU_WIDTH + 2 * POOL_WIDTH + 2 * D_MODEL
EPS = 1e-6

kernel_name = "hybrid_rglru_pool_gated_merge"


def rmsnorm(x, g):
    x32 = x.astype(jnp.float32)
    ms = jnp.mean(x32 * x32, axis=-1, keepdims=True)
    return (x32 * lax.rsqrt(ms + EPS)).astype(x.dtype) * g


def causal_depthwise_conv(x, w, b):
    s = x.shape[1]
    xp = jnp.pad(x, ((0, 0), (CONV_WIDTH - 1, 0), (0, 0)))
    y = b
    for k in range(CONV_WIDTH):
        y = y + xp[:, k:k + s, :] * w[k]
    return y


def block_diag_linear(x, w, b):
    bsz, s, _ = x.shape
    h, dh, _ = w.shape
    xh = x.reshape(bsz, s, h, dh)
    y = jnp.einsum('bshd,hde->bshe', xh, w) + b
    return y.reshape(bsz, s, h * dh)


def rg_lru(x, w_a, b_a, w_x, b_x, lam):
    x32 = x.astype(jnp.float32)
    r = jax.nn.sigmoid(block_diag_linear(x, w_a, b_a).astype(jnp.float32))
    i = jax.nn.sigmoid(block_diag_linear(x, w_x, b_x).astype(jnp.float32))
    log_a = -LRU_C * r * jax.nn.softplus(-lam.astype(jnp.float32))
    a = jnp.exp(log_a)
    mult = jnp.sqrt(-jnp.expm1(2.0 * log_a))
    u = mult * (i * x32)

    def combine(c1, c2):
        a1, b1 = c1
        a2, b2 = c2
        return a2 * a1, a2 * b1 + b2

    _, h = lax.associative_scan(combine, (a, u), axis=1)
    return h.astype(x.dtype)


def multiscale_pool(x, w_pool, scale):
    bsz, s, _ = x.shape
    x32 = x.astype(jnp.float32)
    c = jnp.cumsum(x32, axis=1)
    c_pad = jnp.pad(c, ((0, 0), (MAX_WIN, 0), (0, 0)))
    pos = jnp.arange(s)
    outs = []
    for g, k in enumerate(POOL_WINDOWS):
        cg = c_pad[..., g * POOL_GROUP_DIM:(g + 1) * POOL_GROUP_DIM]
        win_sum = cg[:, MAX_WIN:, :] - cg[:, MAX_WIN - k:MAX_WIN - k + s, :]
        count = jnp.minimum(pos + 1, k).astype(jnp.float32)[None, :, None]
        outs.append(win_sum / count)
    pooled = jnp.concatenate(outs, axis=-1)
    diff = (pooled - x32).astype(x.dtype).reshape(bsz, s, POOL_GROUPS, POOL_GROUP_DIM)
    y = jnp.einsum('bsgd,gde->bsge', diff, w_pool).reshape(bsz, s, POOL_WIDTH)
    return y * scale


def setup_inputs(seed: int = 0) -> dict:
    key = jax.random.key(seed)
    ks = jax.random.split(key, 24)
    f32 = jnp.float32

    def nrm(k, shape, fan_in):
        return jax.random.normal(k, shape, f32) * (fan_in ** -0.5)

    x = jax.random.normal(ks[0], (BATCH, SEQ, D_MODEL), f32)
    p = jax.random.normal(ks[1], (DEPTH, BATCH, SEQ, P_DIM), f32)
    norm_g = 1.0 + 0.05 * jax.random.normal(ks[2], (DEPTH, D_MODEL), f32)
    w_in = nrm(ks[3], (DEPTH, D_MODEL, IN_COLS), D_MODEL)
    conv_w = nrm(ks[4], (DEPTH, CONV_WIDTH, LRU_WIDTH), CONV_WIDTH)
    conv_b = 0.02 * jax.random.normal(ks[5], (DEPTH, LRU_WIDTH), f32)
    lru_w_a = nrm(ks[6], (DEPTH, LRU_HEADS, LRU_HEAD_DIM, LRU_HEAD_DIM), LRU_HEAD_DIM)
    lru_b_a = 0.02 * jax.random.normal(ks[7], (DEPTH, LRU_HEADS, LRU_HEAD_DIM), f32)
    lru_w_x = nrm(ks[8], (DEPTH, LRU_HEADS, LRU_HEAD_DIM, LRU_HEAD_DIM), LRU_HEAD_DIM)
    lru_b_x = 0.02 * jax.random.normal(ks[9], (DEPTH, LRU_HEADS, LRU_HEAD_DIM), f32)
    u = jax.random.uniform(ks[10], (DEPTH, LRU_WIDTH), f32, 0.9, 0.999)
    sa = u ** (1.0 / LRU_C)
    lru_lambda = jnp.log(sa) - jnp.log1p(-sa)
    pool_w = nrm(ks[11], (DEPTH, POOL_GROUPS, POOL_GROUP_DIM, POOL_GROUP_DIM), POOL_GROUP_DIM)
    pool_scale = 1.0 + 0.1 * jax.random.normal(ks[12], (DEPTH, POOL_WIDTH), f32)
    w_proj_lru = nrm(ks[13], (DEPTH, LRU_WIDTH, D_MODEL), LRU_WIDTH)
    w_proj_pool = nrm(ks[14], (DEPTH, POOL_WIDTH, D_MODEL), POOL_WIDTH)
    w_out = nrm(ks[15], (DEPTH, D_MODEL, D_MODEL), D_MODEL)
    ple_norm_g = 1.0 + 0.05 * jax.random.normal(ks[16], (DEPTH, D_MODEL), f32)
    w_ple_gate = nrm(ks[17], (DEPTH, D_MODEL, D_MODEL), D_MODEL)
    w_ple_proj = nrm(ks[18], (DEPTH, P_DIM, D_MODEL), P_DIM)
    final_g = 1.0 + 0.05 * jax.random.normal(ks[19], (D_MODEL,), f32)
    return {
        "x": x, "p": p, "norm_g": norm_g, "w_in": w_in,
        "conv_w": conv_w, "conv_b": conv_b,
        "lru_w_a": lru_w_a, "lru_b_a": lru_b_a, "lru_w_x": lru_w_x, "lru_b_x": lru_b_x,
        "lru_lambda": lru_lambda, "pool_w": pool_w, "pool_scale": pool_scale,
        "w_proj_lru": w_proj_lru, "w_proj_pool": w_proj_pool, "w_out": w_out,
        "ple_norm_g": ple_norm_g, "w_ple_gate": w_ple_gate, "w_ple_proj": w_ple_proj,
        "final_g": final_g,
    }


def reference(x, p, norm_g, w_in, conv_w, conv_b, lru_w_a, lru_b_a, lru_w_x, lru_b_x,
              lru_lambda, pool_w, pool_scale, w_proj_lru, w_proj_pool, w_out,
              ple_norm_g, w_ple_gate, w_ple_proj, final_g):
    split_points = np.cumsum([LRU_WIDTH, LRU_WIDTH, POOL_WIDTH, POOL_WIDTH, D_MODEL]).tolist()
    for i in range(DEPTH):
        h = rmsnorm(x, norm_g[i])
        z = h @ w_in[i]
        xa, ga, xb, gb, ma, mb = jnp.split(z, split_points, axis=-1)
        xa = causal_depthwise_conv(xa, conv_w[i], conv_b[i])
        ya = rg_lru(xa, lru_w_a[i], lru_b_a[i], lru_w_x[i], lru_b_x[i], lru_lambda[i]) * jax.nn.silu(ga)
        yb = multiscale_pool(xb, pool_w[i], pool_scale[i]) * jax.nn.silu(gb)
        merged = jax.nn.sigmoid(ma) * (ya @ w_proj_lru[i]) + jax.nn.sigmoid(mb) * (yb @ w_proj_pool[i])
        x = x + merged @ w_out[i]
        gate = jax.nn.sigmoid(rmsnorm(x, ple_norm_g[i]) @ w_ple_gate[i])
        x = x + gate * (p[i] @ w_ple_proj[i])
    return rmsnorm(x, final_g)
```

```python
import numpy as np
from contextlib import ExitStack
import concourse.bass as bass
import concourse.mybir as mybir
from concourse.bass_utils import run_bass_kernel_spmd

F32 = mybir.dt.float32
BF16 = mybir.dt.bfloat16
AF = mybir.ActivationFunctionType
ALU = mybir.AluOpType

NCORES = 8
SEQ = 2048
D = 1024
PD = 256
T = 128
NSEQ = 2
NT = NSEQ * SEQ // T
EPS = 1e-6
O_XA, O_GA, O_XB, O_GB, O_MA, O_MB = 0, 1024, 2048, 2560, 3072, 4096
WINS = (2, 4, 8, 16)
CW, CB, BA, BX, LAM, PSC, G1, G2, HBA, HBX, CH, C1, MH, HPSC, TMP = 0, 32, 40, 48, 56, 64, 68, 76, 84, 92, 100, 108, 116, 117, 121


class Sched:
    def __init__(self, nc, stack):
        self.nc = nc
        self.stack = stack
        self.lists = {k: [] for k in ("pe", "act", "dve", "pool", "sp")}
        self.sem = {}
        self.cnt = {}
        for k in ("pe", "act", "dve", "pool"):
            self.sem[k] = stack.enter_context(nc.semaphore("s_" + k))
            self.cnt[k] = 0
        self.lastw = {}
        self.readers = {}
        self.waited = {}

    def dsem(self, name):
        if name not in self.sem:
            self.sem[name] = self.stack.enter_context(self.nc.semaphore("d_" + name))
            self.cnt[name] = 0
        return name

    def _waits(self, e, reads, writes):
        deps = {}

        def add(tok, raw):
            if tok is None:
                return
            s, v = tok
            if deps.get(s, 0) < v:
                deps[s] = v

        for k in reads:
            add(self.lastw.get(k), True)
        for k in writes:
            add(self.lastw.get(k), False)
            for s, v in self.readers.get(k, {}).items():
                add((s, v), False)
        out = []
        for s, v in deps.items():
            if self.waited.get((e, s), 0) < v:
                self.waited[(e, s)] = v
                out.append((self.sem[s], v))
        return out

    def _record(self, tok, reads, writes):
        s, v = tok
        for k in reads:
            r = self.readers.setdefault(k, {})
            if r.get(s, 0) < v:
                r[s] = v
        for k in writes:
            self.lastw[k] = tok
            self.readers[k] = {}

    def op(self, e, reads, writes, fn):
        waits = self._waits(e, reads, writes)
        self.cnt[e] += 1
        sem = self.sem[e]

        def run(eng, waits=waits, fn=fn, sem=sem):
            for s, v in waits:
                eng.wait_ge(s, v)
            fn(eng).then_inc(sem, 1)

        self.lists[e].append(run)
        tok = (e, self.cnt[e])
        self._record(tok, reads, writes)
        return tok

    def dma(self, q, semname, reads, writes, fn, record=True):
        self.dsem(semname)
        waits = self._waits(q, reads, writes)
        self.cnt[semname] += 16
        sem = self.sem[semname]

        def run(eng, waits=waits, fn=fn, sem=sem):
            for s, v in waits:
                eng.wait_ge(s, v)
            fn(eng).then_inc(sem, 16)

        self.lists[q].append(run)
        tok = (semname, self.cnt[semname])
        if record:
            self._record(tok, reads, writes)
        return tok

    def barrier(self):
        for e in self.lists:
            waits = []
            for s, c in self.cnt.items():
                if c > 0 and self.waited.get((e, s), 0) < c:
                    self.waited[(e, s)] = c
                    waits.append((self.sem[s], c))

            def run(eng, waits=waits):
                for s, v in waits:
                    eng.wait_ge(s, v)

            self.lists[e].append(run)

    def flush(self):
        nc = self.nc
        lists = self.lists
        with nc.Block() as block:
            @block.tensor
            def _(eng):
                for f in lists["pe"]:
                    f(eng)

            @block.scalar
            def _(eng):
                for f in lists["act"]:
                    f(eng)

            @block.vector
            def _(eng):
                for f in lists["dve"]:
                    f(eng)

            @block.gpsimd
            def _(eng):
                for f in lists["pool"]:
                    f(eng)

            @block.sync
            def _(eng):
                for f in lists["sp"]:
                    f(eng)
        self.lists = {k: [] for k in lists}


def build_program(debug=False):
    nc = bass.Bass("TRN2", target_bir_lowering=False)
    dt = nc.dram_tensor
    x_d = dt("x", [NSEQ, SEQ, D], F32, kind="ExternalInput").ap()
    p_d = dt("p", [NSEQ, SEQ, PD], F32, kind="ExternalInput").ap()
    y_d = dt("y", [NSEQ, SEQ, D], F32, kind="ExternalOutput").ap()
    norm_g_d = dt("norm_g", [D], F32, kind="ExternalInput").ap()
    w_in_d = dt("w_in", [D, 5120], F32, kind="ExternalInput").ap()
    conv_w_d = dt("conv_w", [4, D], F32, kind="ExternalInput").ap()
    conv_b_d = dt("conv_b", [D], F32, kind="ExternalInput").ap()
    lru_w_a_d = dt("lru_w_a", [8, 128, 128], F32, kind="ExternalInput").ap()
    lru_b_a_d = dt("lru_b_a", [D], F32, kind="ExternalInput").ap()
    lru_w_x_d = dt("lru_w_x", [8, 128, 128], F32, kind="ExternalInput").ap()
    lru_b_x_d = dt("lru_b_x", [D], F32, kind="ExternalInput").ap()
    lam_d = dt("lru_lambda", [D], F32, kind="ExternalInput").ap()
    pool_w_d = dt("pool_w", [4, 128, 128], F32, kind="ExternalInput").ap()
    pool_scale_d = dt("pool_scale", [512], F32, kind="ExternalInput").ap()
    w_pl_d = dt("w_proj_lru", [D, D], F32, kind="ExternalInput").ap()
    w_pp_d = dt("w_proj_pool", [512, D], F32, kind="ExternalInput").ap()
    w_out_d = dt("w_out", [D, D], F32, kind="ExternalInput").ap()
    ple_g_d = dt("ple_norm_g", [D], F32, kind="ExternalInput").ap()
    w_pg_d = dt("w_ple_gate", [D, D], F32, kind="ExternalInput").ap()
    w_pe_d = dt("w_ple_proj", [PD, D], F32, kind="ExternalInput").ap()
    fg_d = dt("final_g", [D], F32, kind="ExternalInput").ap()
    m_scr = dt("m_scr", [NT, 128, D], BF16).ap()
    scr_wout = dt("scr_wout", [128, 8 * D], BF16).ap()
    scr_wpg = dt("scr_wpg", [128, 8 * D], BF16).ap()
    scr_wpe = dt("scr_wpe", [128, 2 * D], BF16).ap()
    dbg_d = dt("dbg", [8, 128, D], F32, kind="ExternalOutput").ap() if debug else None

    with ExitStack() as st:
        S = Sched(nc, st)
        sb = lambda name, shape, dty: st.enter_context(nc.sbuf_tensor(name, shape, dty))
        pst = lambda name, shape, dty: st.enter_context(nc.psum_tensor(name, shape, dty))
        w_in = sb("w_in_sb", [128, 8, 5120], BF16)
        w_pl = sb("w_pl_sb", [128, 8, D], BF16)
        w_pp = sb("w_pp_sb", [128, 4, D], BF16)
        w_a = sb("w_a_sb", [128, 8, 128], BF16)
        w_x = sb("w_x_sb", [128, 8, 128], BF16)
        pw = sb("pw_sb", [128, 4, 128], BF16)
        cst = sb("cst", [128, 160], F32)
        rk = sb("rk", [128, 4, 15], F32)
        ident = sb("ident", [128, 128], BF16)
        stats = sb("stats", [128, 6, NT], F32)
        bank_i = [0]
        NB = [7]

        def nbank():
            b = bank_i[0] % NB[0]
            bank_i[0] += 1
            return b

        def col(c):
            return cst[:, c:c + 1]

        def cload(name, src_ap, c0, n):
            S.dma("sp", "cst", [], [("cst", name)],
                  lambda e: e.dma_start(out=cst[:, c0:c0 + n], in_=src_ap, allow_slow_non_contiguous=True))

        def vec8(v):
            return v.rearrange("(c p) -> p c", p=128)

        S.op("pool", [], [("cst", "mh")], lambda e: e.memset(cst[:, MH:MH + 1], -0.5))
        CALL = [("cst", n) for n in ("cw", "cb", "ba", "bx", "lam", "psc")]
        def const_setup(E2, ZA, IDF, PS):
            stg = lambda sl: E2[0:8, sl, :]
            S.dma("sp", "cstA", [], [("e2",)],
                  lambda e: e.dma_start(out=E2[0:8, 0:4, :], in_=conv_w_d.rearrange("t (c p) -> c t p", p=128)))
            for sl, src in ((4, conv_b_d), (5, lru_b_a_d), (6, lru_b_x_d), (7, lam_d)):
                S.dma("sp", "cstA", [], [("e2",)],
                      lambda e, sl=sl, src=src: e.dma_start(out=E2[0:8, sl, :], in_=src.rearrange("(c p) -> c p", p=128)))
            S.dma("sp", "cstA", [], [("za",)],
                  lambda e: e.dma_start(out=ZA[0:4, 0, :], in_=pool_scale_d.rearrange("(c p) -> c p", p=128)))
            tokA = ("cstA", S.cnt["cstA"])
            S.lastw[("e2",)] = tokA
            S.lastw[("za",)] = tokA
            bk = nbank()

            def ftr(e):
                for sl in range(8):
                    e.transpose(PS[bk][:, sl * 8:(sl + 1) * 8], E2[0:8, sl, :], IDF[0:8, 0:8])
                return e.transpose(PS[bk][:, 64:68], ZA[0:4, 0, :], IDF[0:4, 0:4])

            S.op("pe", [("e2",), ("za",), ("identf",)], [("ps", bk)], ftr)
            S.op("dve", [("ps", bk)], [("cst", n) for n in ("cw", "cb", "ba", "bx", "lam", "psc")],
                 lambda e: e.tensor_copy(out=cst[:, 0:68], in_=PS[bk][:, 0:68]))

            S.op("dve", CALL, [("cst", "hba")],
                 lambda e: e.tensor_scalar(out=cst[:, HBA:HBA + 16], in0=cst[:, BA:BA + 16], scalar1=0.5, scalar2=None,
                                           op0=ALU.mult))
            S.op("dve", CALL, [("cst", "hpsc")],
                 lambda e: e.tensor_scalar(out=cst[:, HPSC:HPSC + 4], in0=cst[:, PSC:PSC + 4], scalar1=0.5, scalar2=None,
                                           op0=ALU.mult))
            S.op("act", CALL, [("cst", "tmp")],
                 lambda e: e.activation(out=cst[:, TMP:TMP + 8], in_=cst[:, LAM:LAM + 8], func=AF.Exp, scale=-1.0))
            S.op("act", [("cst", "tmp")], [("cst", "tmp")],
                 lambda e: e.activation(out=cst[:, TMP:TMP + 8], in_=cst[:, TMP:TMP + 8], func=AF.Ln, bias=1.0))
            S.op("dve", [("cst", "tmp")], [("cst", "ch")],
                 lambda e: e.tensor_scalar(out=cst[:, CH:CH + 8], in0=cst[:, TMP:TMP + 8], scalar1=-4.0, scalar2=None,
                                           op0=ALU.mult))
            S.op("dve", [("cst", "tmp")], [("cst", "c1")],
                 lambda e: e.tensor_scalar(out=cst[:, C1:C1 + 8], in0=cst[:, TMP:TMP + 8], scalar1=-8.0, scalar2=None,
                                           op0=ALU.mult))
            for g, kw in enumerate(WINS):
                for pos in range(15):
                    val = 1.0 / (pos + 1) if pos < kw - 1 else 1.0 / kw
                    S.op("pool", [], [("rk",)], lambda e, g=g, pos=pos, val=val: e.memset(rk[:, g, pos:pos + 1], val))
        CONST = CALL + [("cst", n) for n in ("hba", "hpsc", "ch", "c1", "mh")] + [("rk",)]

        def wload(name, dst_ap, src_ap):
            S.dma("pool", "w_" + name, [], [("w", name)], lambda e: e.dma_start(out=dst_ap, in_=src_ap))

        def wblock(blk):
            c0 = blk * 512
            S.dma("pool", "w_in%d" % blk, [], [("w", "in", blk)],
                  lambda e: e.dma_start(out=w_in[:, :, c0:c0 + 512],
                                        in_=w_in_d[:, c0:c0 + 512].rearrange("(k p) n -> p k n", p=128)))

        for blk in (0, 1, 4, 2, 3, 5):
            wblock(blk)
        wload("pw", pw[:], pool_w_d.rearrange("g d e -> d g e"))
        wload("a", w_a[:], lru_w_a_d.rearrange("h d e -> d h e"))
        wload("x", w_x[:], lru_w_x_d.rearrange("h d e -> d h e"))
        for blk in (6, 8, 7, 9):
            wblock(blk)
        wload("pl", w_pl[:], w_pl_d.rearrange("(k p) n -> p k n", p=128))
        wload("pp", w_pp[:], w_pp_d.rearrange("(k p) n -> p k n", p=128))
        S.dma("pool", "scr_w", [], [("scrw", "out")],
              lambda e: e.dma_start(out=scr_wout.rearrange("p (k n) -> p k n", k=8),
                                    in_=w_out_d.rearrange("(k p) n -> p k n", p=128)))
        S.dma("pool", "scr_w", [], [("scrw", "pe")],
              lambda e: e.dma_start(out=scr_wpe.rearrange("p (k n) -> p k n", k=2),
                                    in_=w_pe_d.rearrange("(k p) n -> p k n", p=128)))
        S.dma("pool", "scr_w", [], [("scrw", "pg")],
              lambda e: e.dma_start(out=scr_wpg.rearrange("p (k n) -> p k n", k=8),
                                    in_=w_pg_d.rearrange("(k p) n -> p k n", p=128)))
        scrw_tok = ("scr_w", S.cnt["scr_w"])
        for kk in ("out", "pe", "pg"):
            S.lastw[("scrw", kk)] = scrw_tok

        with ExitStack() as s1:
            sb1 = lambda name, shape, dty: s1.enter_context(nc.sbuf_tensor(name, shape, dty))
            tp0 = s1.enter_context(nc.psum_tensor("tpA", [128, 1024], BF16))
            tp = [tp0, tp0]
            ps = [s1.enter_context(nc.psum_tensor("psA%d" % i, [128, 512], F32)) for i in range(7)]
            NB[0] = 7
            identf = sb1("identf", [128, 128], F32)
            dg = sb1("dg", [128, 4, 8, 128], BF16)
            xin = sb1("xin", [128, D], F32)
            g1b = sb1("g1b", [128, D], F32)
            hb = [sb1("hb%d" % i, [128, D], BF16) for i in range(2)]
            hT = [sb1("hT%d" % i, [128, 8, 128], BF16) for i in range(2)]
            xsb = sb1("xsb", [128, 8, 131], BF16)
            xc = [sb1("xc%d" % i, [128, 8, 128], F32) for i in range(2)]
            xcb = [sb1("xcb%d" % i, [128, 8, 128], BF16) for i in range(2)]
            thr = sb1("thr", [128, 8, 128], F32)
            thi = sb1("thi", [128, 8, 128], F32)
            za = sb1("za", [128, 8, 128], F32)
            e2 = sb1("e2", [128, 8, 128], F32)
            thg2 = [sb1("thg%d" % i, [128, 8, 128], F32) for i in range(2)]
            sgb = sb1("sgb", [128, 4, 128], F32)
            xbs = sb1("xbs", [128, 4, 143], F32)
            pb2 = sb1("pb2", [128, 4, 143], F32)
            pb3 = sb1("pb3", [128, 4, 143], F32)
            diffb = sb1("diffb", [128, 4, 128], BF16)
            halo_a = sb1("halo_a", [128, NSEQ, 8, 3], BF16)
            halo_b = sb1("halo_b", [128, NSEQ, 4, 15], F32)
            carry = sb1("carry", [128, NSEQ, 8], F32)
            yaT = [sb1("yaT%d" % i, [128, 8, 128], BF16) for i in range(2)]
            ybT = [sb1("ybT%d" % i, [128, 4, 128], BF16) for i in range(2)]
            tha = sb1("tha", [128, 8, 128], F32)
            thb = sb1("thb", [128, 8, 128], F32)
            mT = [sb1("mT%d" % i, [128, 8, 128], BF16) for i in range(2)]

            flat = lambda t: t[:].rearrange("p c t -> p (c t)")

            S.dma("sp", "g1b", [], [("g1b",)], lambda e: e.dma_start(out=g1b[:], in_=norm_g_d.partition_broadcast(128)))
            S.op("pool", [], [("identf",)], lambda e: e.memset(identf[:], 0.0))
            S.op("pool", [("identf",)], [("identf",)],
                 lambda e: e.affine_select(out=identf[:], in_=identf[:], pattern=[[-1, 128]],
                                           compare_op=ALU.not_equal, fill=1.0, base=0, channel_multiplier=1))
            S.op("dve", [("identf",)], [("ident",)], lambda e: e.tensor_copy(out=ident[:], in_=identf[:]))
            def late_setup():
                const_setup(e2, za, identf, ps)
                for tap in range(4):
                    for c in range(8):
                        S.op("dve", [("identf",), ("cst", "cw")], [("dg",)],
                             lambda e, tap=tap, c=c: e.tensor_scalar(out=dg[:, tap, c, :], in0=identf[:],
                                                                     scalar1=col(CW + tap * 8 + c), scalar2=None,
                                                                     op0=ALU.mult))
                S.op("pool", [], [("pb2",)], lambda e: e.memset(pb2[:], 0.0))
                S.op("pool", [], [("pb3",)], lambda e: e.memset(pb3[:], 0.0))
                S.op("pool", [], [("halo_a", s) for s in range(NSEQ)], lambda e: e.memset(halo_a[:], 0.0))
                S.op("pool", [], [("halo_b", s) for s in range(NSEQ)], lambda e: e.memset(halo_b[:], 0.0))
                S.op("pool", [], [("carry", s, c) for s in range(NSEQ) for c in range(8)],
                     lambda e: e.memset(carry[:], 0.0))


            def rows(j):
                seq, tt = j % NSEQ, j // NSEQ
                return seq, tt * T

            def rstd_ops(which, j, src_key):
                ssc = stats[:, which, j:j + 1]
                rsc = stats[:, which + 1, j:j + 1]
                S.op("pool", [src_key], [("rs", which, j)],
                     lambda e: e.tensor_scalar(out=rsc, in0=ssc, scalar1=1.0 / D, scalar2=EPS, op0=ALU.mult,
                                               op1=ALU.add))
                S.op("pool", [("rs", which, j), ("cst", "mh")], [("rs", which, j)],
                     lambda e: e.tensor_tensor(out=rsc, in0=rsc, in1=col(MH), op=ALU.pow))
                return rsc

            def a_elem(j):
                seq, r0 = rows(j)
                b = j % 2
                S.dma("sp", "xin", [], [("xin",)], lambda e: e.dma_start(out=xin[:], in_=x_d[seq, r0:r0 + T, :]))
                S.op("act", [("xin",)], [("hb", b), ("ss", 0, j)],
                     lambda e: e.activation(out=hb[b][:], in_=xin[:], func=AF.Square,
                                            accum_out=stats[:, 0, j:j + 1]))
                rsc = rstd_ops(0, j, ("ss", 0, j))
                S.op("dve", [("xin",), ("rs", 0, j), ("g1b",)], [("hb", b)],
                     lambda e: e.scalar_tensor_tensor(out=hb[b][:], in0=xin[:], scalar=rsc, in1=g1b[:], op0=ALU.mult,
                                                      op1=ALU.mult))

            def a_tr(j):
                b = j % 2

                def tr(e):
                    ins = None
                    for k in range(8):
                        ins = e.transpose(tp[b][:, k * 128:(k + 1) * 128], hb[b][:, k * 128:(k + 1) * 128], ident[:])
                    return ins

                S.op("pe", [("hb", b), ("ident",)], [("tp", 0)], tr)
                S.op("act", [("tp", 0)], [("hT", b)],
                     lambda e: e.activation(out=flat(hT[b]), in_=tp[b][:], func=AF.Copy))

            def win_quad(j, col0, bk, n=4):
                b = j % 2

                def f(e):
                    ins = None
                    for q in range(n):
                        for k in range(8):
                            ins = e.matmul(ps[bk][:, q * 128:(q + 1) * 128],
                                           lhsT=w_in[:, k, col0 + q * 128:col0 + (q + 1) * 128],
                                           rhs=hT[b][:, k, :], start=(k == 0), stop=(k == 7))
                    return ins
                return f

            def f_xa(j):
                seq, _ = rows(j)
                b = j % 2
                for Q in range(2):
                    bk = nbank()
                    S.op("pe", [("w", "in", Q), ("hT", b)], [("ps", bk)], win_quad(j, O_XA + Q * 512, bk))
                    S.op("act", [("ps", bk)], [("xsb", Q)],
                         lambda e, bk=bk, Q=Q: e.activation(
                             out=xsb[:, 4 * Q:4 * Q + 4, 3:131],
                             in_=ps[bk][:].rearrange("p (c t) -> p c t", c=4), func=AF.Copy))
                S.op("pool", [("halo_a", seq)], [("xsb_h",)],
                     lambda e: e.tensor_copy(out=xsb[:, :, 0:3], in_=halo_a[:, seq, :, :]))
                S.op("pool", [("xsb", 0), ("xsb", 1)], [("halo_a", seq)],
                     lambda e: e.tensor_copy(out=halo_a[:, seq, :, :], in_=xsb[:, :, 128:131]))

            def f_xb(j):
                seq, r0 = rows(j)
                b = j % 2
                first = (r0 == 0)
                bk = nbank()
                S.op("pe", [("w", "in", 4), ("hT", b)], [("ps", bk)], win_quad(j, O_XB, bk))
                S.op("act", [("ps", bk)], [("xbs",)],
                     lambda e: e.activation(out=xbs[:, :, 15:143], in_=ps[bk][:].rearrange("p (c t) -> p c t", c=4),
                                            func=AF.Copy))
                S.op("pool", [("halo_b", seq)], [("xbs_h",)],
                     lambda e: e.tensor_copy(out=xbs[:, :, 0:15], in_=halo_b[:, seq, :, :]))
                S.op("dve", [("xbs",), ("xbs_h",)], [("pb2",)],
                     lambda e: e.tensor_tensor(out=pb2[:, 0:4, 1:143], in0=xbs[:, 0:4, 1:143], in1=xbs[:, 0:4, 0:142],
                                               op=ALU.add))
                S.op("dve", [("pb2",)], [("pb3",)],
                     lambda e: e.tensor_tensor(out=pb3[:, 1:4, 2:143], in0=pb2[:, 1:4, 2:143], in1=pb2[:, 1:4, 0:141],
                                               op=ALU.add))
                S.op("dve", [("pb3",), ("pb2",)], [("pb2",)],
                     lambda e: e.tensor_tensor(out=pb2[:, 2:4, 4:143], in0=pb3[:, 2:4, 4:143], in1=pb3[:, 2:4, 0:139],
                                               op=ALU.add))
                S.op("dve", [("pb2",), ("pb3",)], [("pb3",)],
                     lambda e: e.tensor_tensor(out=pb3[:, 3, 8:143], in0=pb2[:, 3, 8:143], in1=pb2[:, 3, 0:135],
                                               op=ALU.add))
                for g, kw in enumerate(WINS):
                    src = pb2 if g % 2 == 0 else pb3
                    if first:
                        S.op("dve", [("pb2",), ("pb3",), ("rk",)], [("pb2",), ("pb3",)],
                             lambda e, src=src, g=g: e.tensor_tensor(out=src[:, g, 15:30], in0=src[:, g, 15:30],
                                                                     in1=rk[:, g, :], op=ALU.mult))
                        S.op("dve", [("pb2",), ("pb3",), ("xbs",)], [("diffb", g)],
                             lambda e, src=src, g=g, kw=kw: e.scalar_tensor_tensor(
                                 out=diffb[:, g, 15:128], in0=src[:, g, 30:143], scalar=1.0 / kw,
                                 in1=xbs[:, g, 30:143], op0=ALU.mult, op1=ALU.subtract))
                        S.op("dve", [("pb2",), ("pb3",), ("xbs",)], [("diffb", g)],
                             lambda e, src=src, g=g: e.tensor_tensor(out=diffb[:, g, 0:15], in0=src[:, g, 15:30],
                                                                     in1=xbs[:, g, 15:30], op=ALU.subtract))
                    else:
                        S.op("dve", [("pb2",), ("pb3",), ("xbs",)], [("diffb", g)],
                             lambda e, src=src, g=g, kw=kw: e.scalar_tensor_tensor(
                                 out=diffb[:, g, :], in0=src[:, g, 15:143], scalar=1.0 / kw, in1=xbs[:, g, 15:143],
                                 op0=ALU.mult, op1=ALU.subtract))
                S.op("pool", [("xbs",)], [("halo_b", seq)],
                     lambda e: e.tensor_copy(out=halo_b[:, seq, :, :], in_=xbs[:, :, 128:143]))

            def f_gates_in(j):
                b = j % 2
                thg = thg2[j % 2]
                for Q in range(2):
                    bk = nbank()
                    S.op("pe", [("w", "in", 2 + Q), ("hT", b)], [("ps", bk)], win_quad(j, O_GA + Q * 512, bk))
                    S.op("act", [("ps", bk)], [("thg", j % 2, Q)],
                         lambda e, bk=bk, Q=Q: e.activation(
                             out=thg[:, 4 * Q:4 * Q + 4, :].rearrange("p c t -> p (c t)"), in_=ps[bk][:], func=AF.Tanh,
                             scale=0.5))
                    S.op("dve", [("thg", j % 2, Q), ("ps", bk)], [("thg", j % 2, Q)],
                         lambda e, bk=bk, Q=Q: e.scalar_tensor_tensor(
                             out=thg[:, 4 * Q:4 * Q + 4, :].rearrange("p c t -> p (c t)"),
                             in0=thg[:, 4 * Q:4 * Q + 4, :].rearrange("p c t -> p (c t)"), scalar=1.0, in1=ps[bk][:],
                             op0=ALU.add, op1=ALU.mult))
                bk = nbank()
                S.op("pe", [("w", "in", 5), ("hT", b)], [("ps", bk)], win_quad(j, O_GB, bk))
                S.op("act", [("ps", bk)], [("sgb",)],
                     lambda e, bk=bk: e.activation(out=flat(sgb), in_=ps[bk][:], func=AF.Tanh, scale=0.5))
                S.op("dve", [("sgb",), ("ps", bk)], [("sgb",)],
                     lambda e, bk=bk: e.scalar_tensor_tensor(out=flat(sgb), in0=flat(sgb), scalar=1.0, in1=ps[bk][:],
                                                             op0=ALU.add, op1=ALU.mult))

            def f_conv(j):
                jj = j % 2
                for Q in range(2):
                    bk = nbank()

                    def f(e, bk=bk, Q=Q):
                        ins = None
                        for q in range(4):
                            c = 4 * Q + q
                            for tap in range(4):
                                ins = e.matmul(ps[bk][:, q * 128:(q + 1) * 128], lhsT=dg[:, tap, c, :],
                                               rhs=xsb[:, c, tap:tap + 128], start=(tap == 0), stop=(tap == 3))
                        return ins

                    S.op("pe", [("dg",), ("xsb", 0), ("xsb", 1), ("xsb_h",)], [("ps", bk)], f)
                    for q in range(4):
                        c = 4 * Q + q
                        S.op("act", [("ps", bk), ("cst", "cb")], [("xc", jj, c)],
                             lambda e, bk=bk, q=q, c=c: e.activation(out=xc[jj][:, c, :],
                                                                     in_=ps[bk][:, q * 128:(q + 1) * 128],
                                                                     func=AF.Identity, bias=col(CB + c)))
                S.op("act", [("xc", jj, c) for c in range(8)], [("xcb", jj)],
                     lambda e: e.activation(out=flat(xcb[jj]), in_=flat(xc[jj]), func=AF.Copy))

            def f_lru(j):
                seq, _ = rows(j)
                jj, b = j % 2, j % 2
                XC = [("xc", jj, c) for c in range(8)]
                for Q in range(2):
                    br, bi = nbank(), nbank()

                    def f(e, br=br, bi=bi, Q=Q):
                        ins = None
                        for q in range(4):
                            c = 4 * Q + q
                            e.matmul(ps[br][:, q * 128:(q + 1) * 128], lhsT=w_a[:, c, :], rhs=xcb[jj][:, c, :],
                                     start=True, stop=True)
                        for q in range(4):
                            c = 4 * Q + q
                            ins = e.matmul(ps[bi][:, q * 128:(q + 1) * 128], lhsT=w_x[:, c, :], rhs=xcb[jj][:, c, :],
                                           start=True, stop=True)
                        return ins

                    S.op("pe", [("w", "a"), ("w", "x"), ("xcb", jj)], [("ps", br), ("ps", bi)], f)
                    for q in range(4):
                        c = 4 * Q + q
                        S.op("act", [("ps", br), ("cst", "hba")], [("thr", c)],
                             lambda e, br=br, q=q, c=c: e.activation(out=thr[:, c, :],
                                                                     in_=ps[br][:, q * 128:(q + 1) * 128],
                                                                     func=AF.Tanh, scale=0.5, bias=col(HBA + c)))
                    for q in range(4):
                        c = 4 * Q + q
                        S.op("act", [("ps", bi), ("cst", "hba")], [("thi", c)],
                             lambda e, bi=bi, q=q, c=c: e.activation(out=thi[:, c, :],
                                                                     in_=ps[bi][:, q * 128:(q + 1) * 128],
                                                                     func=AF.Tanh, scale=0.5, bias=col(HBX + c)))
                THR = [("thr", c) for c in range(8)]
                S.op("dve", THR + [("cst", "ch")], [("za",)],
                     lambda e: e.scalar_tensor_tensor(out=za[:], in0=thr[:], scalar=1.0,
                                                      in1=cst[:, CH:CH + 8].unsqueeze(2).broadcast_to([128, 8, 128]),
                                                      op0=ALU.add, op1=ALU.mult))

            def f_lru2(j):
                jj = j % 2
                XC = [("xc", jj, c) for c in range(8)]
                THI = [("thi", c) for c in range(8)]
                S.op("act", [("za",)], [("e2",)] + [("h", c) for c in range(8)],
                     lambda e: e.activation(out=flat(e2), in_=flat(za), func=AF.Exp, scale=2.0))
                S.op("pool", [("e2",)], [("e2",)],
                     lambda e: e.tensor_scalar(out=flat(e2), in0=flat(e2), scalar1=1.0, scalar2=0.0, op0=ALU.min,
                                               op1=ALU.max))
                S.op("act", [("za",)], [("za",)],
                     lambda e: e.activation(out=flat(za), in_=flat(za), func=AF.Exp))
                S.op("dve", THI + XC, XC,
                     lambda e: e.scalar_tensor_tensor(out=flat(xc[jj]), in0=flat(thi), scalar=1.0, in1=flat(xc[jj]),
                                                      op0=ALU.add, op1=ALU.mult))

            def f_lru3(j):
                seq, _ = rows(j)
                jj, b = j % 2, j % 2
                XC = [("xc", jj, c) for c in range(8)]
                S.op("act", [("e2",)], [("e2",)],
                     lambda e: e.activation(out=flat(e2), in_=flat(e2), func=AF.Sqrt, scale=-1.0, bias=1.0))

            def f_lru3b(j):
                seq, _ = rows(j)
                jj, b = j % 2, j % 2
                XC = [("xc", jj, c) for c in range(8)]
                S.op("dve", [("e2",)] + XC, XC,
                     lambda e: e.scalar_tensor_tensor(out=flat(xc[jj]), in0=flat(xc[jj]), scalar=0.5, in1=flat(e2),
                                                      op0=ALU.mult, op1=ALU.mult))
                for c in range(8):
                    S.op("dve", [("za",), ("xc", jj, c), ("carry", seq, c), ("e2",)], [("h", c)],
                         lambda e, c=c: e.tensor_tensor_scan(out=e2[:, c, :], data0=za[:, c, :], data1=xc[jj][:, c, :],
                                                             initial=carry[:, seq, c:c + 1], op0=ALU.mult,
                                                             op1=ALU.add))
                    S.op("pool", [("h", c)], [("carry", seq, c)],
                         lambda e, c=c: e.tensor_copy(out=carry[:, seq, c:c + 1], in_=e2[:, c, 127:128]))
                H = [("h", c) for c in range(8)]
                thg = thg2[j % 2]
                S.op("dve", H + [("thg", j % 2, 0), ("thg", j % 2, 1)], [("yaT", b), ("e2",)],
                     lambda e: e.scalar_tensor_tensor(out=flat(yaT[b]), in0=flat(e2), scalar=0.5, in1=flat(thg),
                                                      op0=ALU.mult, op1=ALU.mult))

            def f_poolw(j):
                b = j % 2
                bk = nbank()

                def f(e):
                    ins = None
                    for g in range(4):
                        ins = e.matmul(ps[bk][:, g * 128:(g + 1) * 128], lhsT=pw[:, g, :], rhs=diffb[:, g, :],
                                       start=True, stop=True)
                    return ins

                S.op("pe", [("w", "pw")] + [("diffb", g) for g in range(4)], [("ps", bk)], f)
                for g in range(4):
                    S.op("dve", [("ps", bk), ("sgb",), ("cst", "hpsc")], [("ybT", b)],
                         lambda e, g=g: e.scalar_tensor_tensor(out=ybT[b][:, g, :], in0=ps[bk][:, g * 128:(g + 1) * 128],
                                                               scalar=col(HPSC + g), in1=sgb[:, g, :], op0=ALU.mult,
                                                               op1=ALU.mult))

            def m_gate(j):
                b = j % 2
                for Q in range(2):
                    for which, off, dst in (("tha", O_MA, tha), ("thb", O_MB, thb)):
                        bk = nbank()
                        S.op("pe", [("w", "in", (off + Q * 512) // 512), ("hT", b)], [("ps", bk)], win_quad(j, off + Q * 512, bk))
                        S.op("act", [("ps", bk)], [(which, Q)],
                             lambda e, bk=bk, Q=Q, dst=dst: e.activation(
                                 out=dst[:, 4 * Q:4 * Q + 4, :].rearrange("p c t -> p (c t)"), in_=ps[bk][:],
                                 func=AF.Tanh, scale=0.5))

            def m_proj(j):
                b = j % 2
                for Q in range(2):
                    ba_, bb_ = nbank(), nbank()

                    def fa(e, ba_=ba_, Q=Q):
                        ins = None
                        for q in range(4):
                            o = 4 * Q + q
                            for k in range(8):
                                ins = e.matmul(ps[ba_][:, q * 128:(q + 1) * 128], lhsT=w_pl[:, k, o * 128:(o + 1) * 128],
                                               rhs=yaT[b][:, k, :], start=(k == 0), stop=(k == 7))
                        return ins

                    def fb(e, bb_=bb_, Q=Q):
                        ins = None
                        for q in range(4):
                            o = 4 * Q + q
                            for k in range(4):
                                ins = e.matmul(ps[bb_][:, q * 128:(q + 1) * 128], lhsT=w_pp[:, k, o * 128:(o + 1) * 128],
                                               rhs=ybT[b][:, k, :], start=(k == 0), stop=(k == 3))
                        return ins

                    S.op("pe", [("w", "pl"), ("yaT", b)], [("ps", ba_)], fa)
                    S.op("pe", [("w", "pp"), ("ybT", b)], [("ps", bb_)], fb)
                    sl = lambda t, Q=Q: t[:, 4 * Q:4 * Q + 4, :].rearrange("p c t -> p (c t)")
                    S.op("dve", [("tha", Q), ("ps", ba_)], [("tha", Q)],
                         lambda e, ba_=ba_, sl=sl: e.scalar_tensor_tensor(out=sl(tha), in0=sl(tha), scalar=1.0,
                                                                          in1=ps[ba_][:], op0=ALU.add, op1=ALU.mult))
                    S.op("dve", [("thb", Q), ("ps", bb_)], [("thb", Q)],
                         lambda e, bb_=bb_, sl=sl: e.scalar_tensor_tensor(out=sl(thb), in0=sl(thb), scalar=1.0,
                                                                          in1=ps[bb_][:], op0=ALU.add, op1=ALU.mult))
                    S.op("dve", [("tha", Q), ("thb", Q)], [("mT", b, Q)],
                         lambda e, sl=sl: e.tensor_tensor(out=sl(mT[b]), in0=sl(tha), in1=sl(thb), op=ALU.add))
                S.dma("sp", "mst%d" % b, [("mT", b, 0), ("mT", b, 1)], [("scr", j)],
                      lambda e: e.dma_start(out=m_scr[j], in_=flat(mT[b])))

            a_elem(0)
            a_tr(0)
            late_setup()
            f_xb(0)
            for j in range(NT + 1):
                if j + 1 < NT:
                    a_elem(j + 1)
                if j < NT:
                    f_xa(j)
                if j >= 1:
                    f_lru3(j - 1)
                if j < NT:
                    f_gates_in(j)
                if j >= 1:
                    f_lru3b(j - 1)
                if j < NT:
                    f_conv(j)
                if j >= 1:
                    m_gate(j - 1)
                if j < NT:
                    f_lru(j)
                if j + 1 < NT:
                    a_tr(j + 1)
                if j < NT:
                    f_lru2(j)
                if j < NT:
                    f_poolw(j)
                if j >= 1:
                    m_proj(j - 1)
                if j + 1 < NT:
                    f_xb(j + 1)
            S.barrier()
            S.flush()

        with ExitStack() as s2:
            sb2 = lambda name, shape, dty: s2.enter_context(nc.sbuf_tensor(name, shape, dty))
            tp = [s2.enter_context(nc.psum_tensor("tpB%d" % i, [128, 1024], BF16)) for i in range(2)]
            ps = [s2.enter_context(nc.psum_tensor("psB%d" % i, [128, 512], F32)) for i in range(6)]
            NB[0] = 6
            bank_i[0] = 0
            w_out = sb2("w_out_sb", [128, 8, D], BF16)
            w_pg = sb2("w_pg_sb", [128, 8, D], BF16)
            w_pe = sb2("w_pe_sb", [128, 2, D], BF16)
            def load_wout(h):
                S.dma("sp", "w2_out%d" % h, [("scrw", "out")], [("w", "out", h)],
                      lambda e: e.dma_start(out=w_out[:, :, h * 512:(h + 1) * 512],
                                            in_=scr_wout.rearrange("p (k n) -> p k n", k=8)[:, :, h * 512:(h + 1) * 512]))

            load_wout(0)
            fg = sb2("fg", [128, D], F32)
            g2b = sb2("g2b", [128, D], F32)
            xr = [sb2("xr%d" % i, [128, D], F32) for i in range(5)]
            mIn = [sb2("mIn%d" % i, [128, 8, 128], BF16) for i in range(3)]
            x1n = [sb2("x1n%d" % i, [128, D], BF16) for i in range(3)]
            junk = sb2("junk", [128, D], BF16)
            x1nT = [sb2("x1nT%d" % i, [128, 8, 128], BF16) for i in range(3)]
            th = [sb2("th%d" % i, [128, 512], F32) for i in range(2)]
            pin = [sb2("pin%d" % i, [128, PD], F32) for i in range(3)]
            pbf = [sb2("pbf%d" % i, [128, PD], BF16) for i in range(3)]
            pT = [sb2("pT%d" % i, [128, 2, 128], BF16) for i in range(3)]


            def rows(j):
                seq, tt = j % NSEQ, j // NSEQ
                return seq, tt * T

            def rstd2(which, j, src_key):
                ssc = stats[:, which, j:j + 1]
                rsc = stats[:, which + 1, j:j + 1]
                S.op("pool", [src_key], [("rs", which, j)],
                     lambda e: e.tensor_scalar(out=rsc, in0=ssc, scalar1=1.0 / D, scalar2=EPS, op0=ALU.mult,
                                               op1=ALU.add))
                S.op("pool", [("rs", which, j), ("cst", "mh")], [("rs", which, j)],
                     lambda e: e.tensor_tensor(out=rsc, in0=rsc, in1=col(MH), op=ALU.pow))
                return rsc

            def load(j):
                seq, r0 = rows(j)
                a, b = j % 5, j % 3
                S.dma("sp", "xr%d" % a, [], [("xr", a, 0), ("xr", a, 1)],
                      lambda e: e.dma_start(out=xr[a][:], in_=x_d[seq, r0:r0 + T, :]))
                S.dma("sp", "mIn%d" % b, [("scr", j)], [("mIn", b)],
                      lambda e: e.dma_start(out=mIn[b][:].rearrange("p k t -> p (k t)"), in_=m_scr[j]))
                S.dma("sp", "pin%d" % b, [], [("pin", b)],
                      lambda e: e.dma_start(out=pin[b][:], in_=p_d[seq, r0:r0 + T, :]))

            def stage_d(j):
                a, b = j % 5, j % 3
                for h in range(2):
                    bk = nbank()

                    def f(e, bk=bk, h=h):
                        ins = None
                        for k in range(8):
                            ins = e.matmul(ps[bk][:], lhsT=mIn[b][:, k, :], rhs=w_out[:, k, h * 512:(h + 1) * 512],
                                           start=(k == 0), stop=(k == 7))
                        return ins

                    S.op("pe", [("w", "out", h), ("mIn", b)], [("ps", bk)], f)
                    S.op("dve", [("ps", bk), ("xr", a, h)], [("xr", a, h)],
                         lambda e, bk=bk, h=h: e.scalar_tensor_tensor(out=xr[a][:, h * 512:(h + 1) * 512], in0=ps[bk][:],
                                                                      scalar=0.5, in1=xr[a][:, h * 512:(h + 1) * 512],
                                                                      op0=ALU.mult, op1=ALU.add))

            def stage_d2(j):
                a, b = j % 5, j % 3
                S.op("act", [("xr", a, 0), ("xr", a, 1)], [("x1n", b), ("ss", 2, j)],
                     lambda e: e.activation(out=x1n[b][:], in_=xr[a][:], func=AF.Square, accum_out=stats[:, 2, j:j + 1]))
                rsc = rstd2(2, j, ("ss", 2, j))
                S.op("dve", [("xr", a, 0), ("xr", a, 1), ("rs", 2, j), ("g2b",)], [("x1n", b)],
                     lambda e: e.scalar_tensor_tensor(out=x1n[b][:], in0=xr[a][:], scalar=rsc, in1=g2b[:], op0=ALU.mult,
                                                      op1=ALU.mult))
                S.op("dve", [("pin", b)], [("pbf", b)], lambda e: e.tensor_copy(out=pbf[b][:], in_=pin[b][:]))

            def stage_t(j):
                b = j % 3

                def tr(e):
                    ins = None
                    for k in range(8):
                        ins = e.transpose(tp[0][:, k * 128:(k + 1) * 128], x1n[b][:, k * 128:(k + 1) * 128], ident[:])
                    return ins

                S.op("pe", [("x1n", b), ("ident",)], [("tp", 0)], tr)
                S.op("act", [("tp", 0)], [("x1nT", b)],
                     lambda e: e.activation(out=x1nT[b][:].rearrange("p k t -> p (k t)"), in_=tp[0][:], func=AF.Copy))

                def trp(e):
                    e.transpose(tp[1][:, 0:128], pbf[b][:, 0:128], ident[:])
                    return e.transpose(tp[1][:, 128:256], pbf[b][:, 128:256], ident[:])

                S.op("pe", [("pbf", b), ("ident",)], [("tp", 1)], trp)
                S.op("act", [("tp", 1)], [("pT", b)],
                     lambda e: e.activation(out=pT[b][:].rearrange("p k t -> p (k t)"), in_=tp[1][:, 0:256], func=AF.Copy))

            def stage_e(j):
                seq, r0 = rows(j)
                a, b = j % 5, j % 3
                for h in range(2):
                    bg = nbank()

                    def fgate(e, bg=bg, h=h):
                        ins = None
                        for k in range(8):
                            ins = e.matmul(ps[bg][:], lhsT=x1nT[b][:, k, :], rhs=w_pg[:, k, h * 512:(h + 1) * 512],
                                           start=(k == 0), stop=(k == 7))
                        return ins

                    S.op("pe", [("w", "pg"), ("x1nT", b)], [("ps", bg)], fgate)
                    S.op("act", [("ps", bg)], [("th", h)],
                         lambda e, bg=bg, h=h: e.activation(out=th[h][:], in_=ps[bg][:], func=AF.Tanh, scale=0.5))
                    be = nbank()

                    def fpe(e, be=be, h=h):
                        e.matmul(ps[be][:], lhsT=pT[b][:, 0, :], rhs=w_pe[:, 0, h * 512:(h + 1) * 512], start=True,
                                 stop=False)
                        return e.matmul(ps[be][:], lhsT=pT[b][:, 1, :], rhs=w_pe[:, 1, h * 512:(h + 1) * 512],
                                        start=False, stop=True)

                    S.op("pe", [("w", "pe"), ("pT", b)], [("ps", be)], fpe)
                    S.op("dve", [("th", h), ("ps", be)], [("th", h)],
                         lambda e, be=be, h=h: e.scalar_tensor_tensor(out=th[h][:], in0=th[h][:], scalar=1.0,
                                                                      in1=ps[be][:], op0=ALU.add, op1=ALU.mult))
                    S.op("dve", [("th", h), ("xr", a, h)], [("xr", a, h)],
                         lambda e, h=h: e.scalar_tensor_tensor(out=xr[a][:, h * 512:(h + 1) * 512], in0=th[h][:],
                                                               scalar=0.5, in1=xr[a][:, h * 512:(h + 1) * 512],
                                                               op0=ALU.mult, op1=ALU.add))

            def stage_e2(j):
                seq, r0 = rows(j)
                a, b = j % 5, j % 3
                S.op("act", [("xr", a, 0), ("xr", a, 1)], [("junk",), ("ss", 4, j)],
                     lambda e: e.activation(out=junk[:], in_=xr[a][:], func=AF.Square, accum_out=stats[:, 4, j:j + 1]))
                rsc = rstd2(4, j, ("ss", 4, j))
                S.op("dve", [("xr", a, 0), ("xr", a, 1), ("rs", 4, j), ("fg",)], [("xr", a, 0), ("xr", a, 1)],
                     lambda e: e.scalar_tensor_tensor(out=xr[a][:], in0=xr[a][:], scalar=rsc, in1=fg[:], op0=ALU.mult,
                                                      op1=ALU.mult))
                S.dma("sp", "yst%d" % a, [("xr", a, 0), ("xr", a, 1)], [("y", j)],
                      lambda e: e.dma_start(out=y_d[seq, r0:r0 + T, :], in_=xr[a][:]))

            load(0)
            load_wout(1)
            S.dma("sp", "g2b", [], [("g2b",)], lambda e: e.dma_start(out=g2b[:], in_=ple_g_d.partition_broadcast(128)))
            load(1)
            S.dma("sp", "w2_pe", [("scrw", "pe")], [("w", "pe")],
                  lambda e: e.dma_start(out=w_pe[:].rearrange("p k n -> p (k n)"), in_=scr_wpe))
            S.dma("sp", "w2_pg", [("scrw", "pg")], [("w", "pg")],
                  lambda e: e.dma_start(out=w_pg[:].rearrange("p k n -> p (k n)"), in_=scr_wpg))
            S.dma("sp", "fg", [], [("fg",)], lambda e: e.dma_start(out=fg[:], in_=fg_d.partition_broadcast(128)))
            for j in range(NT + 2):
                if j + 2 < NT:
                    load(j + 2)
                if j < NT:
                    stage_d(j)
                if j >= 2:
                    stage_e(j - 2)
                if 1 <= j <= NT:
                    stage_t(j - 1)
                if j < NT:
                    stage_d2(j)
                if j >= 2:
                    stage_e2(j - 2)
            S.barrier()
            S.flush()
    return nc


_CACHE = {}


def kernel(**inputs):
    f32 = lambda a: np.ascontiguousarray(np.asarray(a, dtype=np.float32))
    x = f32(inputs["x"])
    p = f32(inputs["p"])[0]
    shared = {
        "norm_g": f32(inputs["norm_g"])[0], "w_in": f32(inputs["w_in"])[0],
        "conv_w": f32(inputs["conv_w"])[0], "conv_b": f32(inputs["conv_b"])[0],
        "lru_w_a": f32(inputs["lru_w_a"])[0], "lru_b_a": f32(inputs["lru_b_a"])[0].reshape(-1),
        "lru_w_x": f32(inputs["lru_w_x"])[0], "lru_b_x": f32(inputs["lru_b_x"])[0].reshape(-1),
        "lru_lambda": f32(inputs["lru_lambda"])[0], "pool_w": f32(inputs["pool_w"])[0],
        "pool_scale": f32(inputs["pool_scale"])[0], "w_proj_lru": f32(inputs["w_proj_lru"])[0],
        "w_proj_pool": f32(inputs["w_proj_pool"])[0], "w_out": f32(inputs["w_out"])[0],
        "ple_norm_g": f32(inputs["ple_norm_g"])[0], "w_ple_gate": f32(inputs["w_ple_gate"])[0],
        "w_ple_proj": f32(inputs["w_ple_proj"])[0], "final_g": f32(inputs["final_g"]),
    }
    shared = {k: np.ascontiguousarray(v) for k, v in shared.items()}
    if "nc" not in _CACHE:
        _CACHE["nc"] = build_program()
    nc = _CACHE["nc"]
    in_maps = []
    for c in range(NCORES):
        m = dict(shared)
        m["x"] = np.ascontiguousarray(x[c * NSEQ:(c + 1) * NSEQ])
        m["p"] = np.ascontiguousarray(p[c * NSEQ:(c + 1) * NSEQ])
        in_maps.append(m)
    res = run_bass_kernel_spmd(nc, in_maps, core_ids=list(range(NCORES)))
    out = np.concatenate([np.asarray(r["y"]) for r in res.results], axis=0)
    return out.astype(np.float32, copy=False)
```

```python
import numpy as np
from contextlib import ExitStack
import concourse.bass as bass
import concourse.mybir as mybir
from concourse.bass_utils import run_bass_kernel_spmd

F32 = mybir.dt.float32
BF16 = mybir.dt.bfloat16
AF = mybir.ActivationFunctionType
ALU = mybir.AluOpType

NCORES = 8
SEQ = 2048
D = 1024
PD = 256
T = 128
NSEQ = 2
NT = NSEQ * SEQ // T
EPS = 1e-6
O_XA, O_GA, O_XB, O_GB, O_MA, O_MB = 0, 1024, 2048, 2560, 3072, 4096
WINS = (2, 4, 8, 16)
CW, CB, BA, BX, LAM, PSC, G1, G2, HBA, HBX, CH, C1, MH, HPSC, TMP = 0, 32, 40, 48, 56, 64, 68, 76, 84, 92, 100, 108, 116, 117, 121


class Sched:
    def __init__(self, nc, stack):
        self.nc = nc
        self.stack = stack
        self.lists = {k: [] for k in ("pe", "act", "dve", "pool", "sp")}
        self.sem = {}
        self.cnt = {}
        for k in ("pe", "act", "dve", "pool"):
            self.sem[k] = stack.enter_context(nc.semaphore("s_" + k))
            self.cnt[k] = 0
        self.lastw = {}
        self.readers = {}
        self.waited = {}

    def dsem(self, name):
        if name not in self.sem:
            self.sem[name] = self.stack.enter_context(self.nc.semaphore("d_" + name))
            self.cnt[name] = 0
        return name

    def _waits(self, e, reads, writes):
        deps = {}

        def add(tok, raw):
            if tok is None:
                return
            s, v = tok
            if deps.get(s, 0) < v:
                deps[s] = v

        for k in reads:
            add(self.lastw.get(k), True)
        for k in writes:
            add(self.lastw.get(k), False)
            for s, v in self.readers.get(k, {}).items():
                add((s, v), False)
        out = []
        for s, v in deps.items():
            if self.waited.get((e, s), 0) < v:
                self.waited[(e, s)] = v
                out.append((self.sem[s], v))
        return out

    def _record(self, tok, reads, writes):
        s, v = tok
        for k in reads:
            r = self.readers.setdefault(k, {})
            if r.get(s, 0) < v:
                r[s] = v
        for k in writes:
            self.lastw[k] = tok
            self.readers[k] = {}

    def op(self, e, reads, writes, fn):
        waits = self._waits(e, reads, writes)
        self.cnt[e] += 1
        sem = self.sem[e]

        def run(eng, waits=waits, fn=fn, sem=sem):
            for s, v in waits:
                eng.wait_ge(s, v)
            fn(eng).then_inc(sem, 1)

        self.lists[e].append(run)
        tok = (e, self.cnt[e])
        self._record(tok, reads, writes)
        return tok

    def dma(self, q, semname, reads, writes, fn, record=True):
        self.dsem(semname)
        waits = self._waits(q, reads, writes)
        self.cnt[semname] += 16
        sem = self.sem[semname]

        def run(eng, waits=waits, fn=fn, sem=sem):
            for s, v in waits:
                eng.wait_ge(s, v)
            fn(eng).then_inc(sem, 16)

        self.lists[q].append(run)
        tok = (semname, self.cnt[semname])
        if record:
            self._record(tok, reads, writes)
        return tok

    def barrier(self):
        for e in self.lists:
            waits = []
            for s, c in self.cnt.items():
                if c > 0 and self.waited.get((e, s), 0) < c:
                    self.waited[(e, s)] = c
                    waits.append((self.sem[s], c))

            def run(eng, waits=waits):
                for s, v in waits:
                    eng.wait_ge(s, v)

            self.lists[e].append(run)

    def flush(self):
        nc = self.nc
        lists = self.lists
        with nc.Block() as block:
            @block.tensor
            def _(eng):
                for f in lists["pe"]:
                    f(eng)

            @block.scalar
            def _(eng):
                for f in lists["act"]:
                    f(eng)

            @block.vector
            def _(eng):
                for f in lists["dve"]:
                    f(eng)

            @block.gpsimd
            def _(eng):
                for f in lists["pool"]:
                    f(eng)

            @block.sync
            def _(eng):
                for f in lists["sp"]:
                    f(eng)
        self.lists = {k: [] for k in lists}


def build_program(debug=False):
    nc = bass.Bass("TRN2", target_bir_lowering=False)
    dt = nc.dram_tensor
    x_d = dt("x", [NSEQ, SEQ, D], F32, kind="ExternalInput").ap()
    p_d = dt("p", [NSEQ, SEQ, PD], F32, kind="ExternalInput").ap()
    y_d = dt("y", [NSEQ, SEQ, D], F32, kind="ExternalOutput").ap()
    norm_g_d = dt("norm_g", [D], F32, kind="ExternalInput").ap()
    w_in_d = dt("w_in", [D, 5120], F32, kind="ExternalInput").ap()
    conv_w_d = dt("conv_w", [4, D], F32, kind="ExternalInput").ap()
    conv_b_d = dt("conv_b", [D], F32, kind="ExternalInput").ap()
    lru_w_a_d = dt("lru_w_a", [8, 128, 128], F32, kind="ExternalInput").ap()
    lru_b_a_d = dt("lru_b_a", [D], F32, kind="ExternalInput").ap()
    lru_w_x_d = dt("lru_w_x", [8, 128, 128], F32, kind="ExternalInput").ap()
    lru_b_x_d = dt("lru_b_x", [D], F32, kind="ExternalInput").ap()
    lam_d = dt("lru_lambda", [D], F32, kind="ExternalInput").ap()
    pool_w_d = dt("pool_w", [4, 128, 128], F32, kind="ExternalInput").ap()
    pool_scale_d = dt("pool_scale", [512], F32, kind="ExternalInput").ap()
    w_pl_d = dt("w_proj_lru", [D, D], F32, kind="ExternalInput").ap()
    w_pp_d = dt("w_proj_pool", [512, D], F32, kind="ExternalInput").ap()
    w_out_d = dt("w_out", [D, D], F32, kind="ExternalInput").ap()
    ple_g_d = dt("ple_norm_g", [D], F32, kind="ExternalInput").ap()
    w_pg_d = dt("w_ple_gate", [D, D], F32, kind="ExternalInput").ap()
    w_pe_d = dt("w_ple_proj", [PD, D], F32, kind="ExternalInput").ap()
    fg_d = dt("final_g", [D], F32, kind="ExternalInput").ap()
    m_scr = dt("m_scr", [NT, 128, D], BF16).ap()
    scr_wout = dt("scr_wout", [128, 8 * D], BF16).ap()
    scr_wpg = dt("scr_wpg", [128, 8 * D], BF16).ap()
    scr_wpe = dt("scr_wpe", [128, 2 * D], BF16).ap()
    dbg_d = dt("dbg", [8, 128, D], F32, kind="ExternalOutput").ap() if debug else None

    with ExitStack() as st:
        S = Sched(nc, st)
        sb = lambda name, shape, dty: st.enter_context(nc.sbuf_tensor(name, shape, dty))
        pst = lambda name, shape, dty: st.enter_context(nc.psum_tensor(name, shape, dty))
        w_in = sb("w_in_sb", [128, 8, 5120], BF16)
        w_pl = sb("w_pl_sb", [128, 8, D], BF16)
        w_pp = sb("w_pp_sb", [128, 4, D], BF16)
        w_a = sb("w_a_sb", [128, 8, 128], BF16)
        w_x = sb("w_x_sb", [128, 8, 128], BF16)
        pw = sb("pw_sb", [128, 4, 128], BF16)
        cst = sb("cst", [128, 160], F32)
        rk = sb("rk", [128, 4, 15], F32)
        ident = sb("ident", [128, 128], BF16)
        stats = sb("stats", [128, 6, NT], F32)
        bank_i = [0]
        NB = [7]

        def nbank():
            b = bank_i[0] % NB[0]
            bank_i[0] += 1
            return b

        def col(c):
            return cst[:, c:c + 1]

        def cload(name, src_ap, c0, n):
            S.dma("sp", "cst", [], [("cst", name)],
                  lambda e: e.dma_start(out=cst[:, c0:c0 + n], in_=src_ap, allow_slow_non_contiguous=True))

        def vec8(v):
            return v.rearrange("(c p) -> p c", p=128)

        S.op("pool", [], [("cst", "mh")], lambda e: e.memset(cst[:, MH:MH + 1], -0.5))
        CALL = [("cst", n) for n in ("cw", "cb", "ba", "bx", "lam", "psc")]
        def const_setup(E2, ZA, IDF, PS):
            stg = lambda sl: E2[0:8, sl, :]
            S.dma("sp", "cstA", [], [("e2",)],
                  lambda e: e.dma_start(out=E2[0:8, 0:4, :], in_=conv_w_d.rearrange("t (c p) -> c t p", p=128)))
            for sl, src in ((4, conv_b_d), (5, lru_b_a_d), (6, lru_b_x_d), (7, lam_d)):
                S.dma("sp", "cstA", [], [("e2",)],
                      lambda e, sl=sl, src=src: e.dma_start(out=E2[0:8, sl, :], in_=src.rearrange("(c p) -> c p", p=128)))
            S.dma("sp", "cstA", [], [("za",)],
                  lambda e: e.dma_start(out=ZA[0:4, 0, :], in_=pool_scale_d.rearrange("(c p) -> c p", p=128)))
            tokA = ("cstA", S.cnt["cstA"])
            S.lastw[("e2",)] = tokA
            S.lastw[("za",)] = tokA
            bk = nbank()

            def ftr(e):
                for sl in range(8):
                    e.transpose(PS[bk][:, sl * 8:(sl + 1) * 8], E2[0:8, sl, :], IDF[0:8, 0:8])
                return e.transpose(PS[bk][:, 64:68], ZA[0:4, 0, :], IDF[0:4, 0:4])

            S.op("pe", [("e2",), ("za",), ("identf",)], [("ps", bk)], ftr)
            S.op("dve", [("ps", bk)], [("cst", n) for n in ("cw", "cb", "ba", "bx", "lam", "psc")],
                 lambda e: e.tensor_copy(out=cst[:, 0:68], in_=PS[bk][:, 0:68]))

            S.op("dve", CALL, [("cst", "hba")],
                 lambda e: e.tensor_scalar(out=cst[:, HBA:HBA + 16], in0=cst[:, BA:BA + 16], scalar1=0.5, scalar2=None,
                                           op0=ALU.mult))
            S.op("dve", CALL, [("cst", "hpsc")],
                 lambda e: e.tensor_scalar(out=cst[:, HPSC:HPSC + 4], in0=cst[:, PSC:PSC + 4], scalar1=0.5, scalar2=None,
                                           op0=ALU.mult))
            S.op("act", CALL, [("cst", "tmp")],
                 lambda e: e.activation(out=cst[:, TMP:TMP + 8], in_=cst[:, LAM:LAM + 8], func=AF.Exp, scale=-1.0))
            S.op("act", [("cst", "tmp")], [("cst", "tmp")],
                 lambda e: e.activation(out=cst[:, TMP:TMP + 8], in_=cst[:, TMP:TMP + 8], func=AF.Ln, bias=1.0))
            S.op("dve", [("cst", "tmp")], [("cst", "ch")],
                 lambda e: e.tensor_scalar(out=cst[:, CH:CH + 8], in0=cst[:, TMP:TMP + 8], scalar1=-4.0, scalar2=None,
                                           op0=ALU.mult))
            S.op("dve", [("cst", "tmp")], [("cst", "c1")],
                 lambda e: e.tensor_scalar(out=cst[:, C1:C1 + 8], in0=cst[:, TMP:TMP + 8], scalar1=-8.0, scalar2=None,
                                           op0=ALU.mult))
            for g, kw in enumerate(WINS):
                for pos in range(15):
                    val = 1.0 / (pos + 1) if pos < kw - 1 else 1.0 / kw
                    S.op("pool", [], [("rk",)], lambda e, g=g, pos=pos, val=val: e.memset(rk[:, g, pos:pos + 1], val))
        CONST = CALL + [("cst", n) for n in ("hba", "hpsc", "ch", "c1", "mh")] + [("rk",)]

        def wload(name, dst_ap, src_ap):
            S.dma("pool", "w_" + name, [], [("w", name)], lambda e: e.dma_start(out=dst_ap, in_=src_ap))

        def wblock(blk):
            c0 = blk * 512
            S.dma("pool", "w_in%d" % blk, [], [("w", "in", blk)],
                  lambda e: e.dma_start(out=w_in[:, :, c0:c0 + 512],
                                        in_=w_in_d[:, c0:c0 + 512].rearrange("(k p) n -> p k n", p=128)))

        for blk in (0, 1, 4, 2, 3, 5):
            wblock(blk)
        wload("pw", pw[:], pool_w_d.rearrange("g d e -> d g e"))
        wload("a", w_a[:], lru_w_a_d.rearrange("h d e -> d h e"))
        wload("x", w_x[:], lru_w_x_d.rearrange("h d e -> d h e"))
        for blk in (6, 8, 7, 9):
            wblock(blk)
        wload("pl", w_pl[:], w_pl_d.rearrange("(k p) n -> p k n", p=128))
        wload("pp", w_pp[:], w_pp_d.rearrange("(k p) n -> p k n", p=128))
        S.dma("pool", "scr_w", [], [("scrw", "out")],
              lambda e: e.dma_start(out=scr_wout.rearrange("p (k n) -> p k n", k=8),
                                    in_=w_out_d.rearrange("(k p) n -> p k n", p=128)))
        S.dma("pool", "scr_w", [], [("scrw", "pe")],
              lambda e: e.dma_start(out=scr_wpe.rearrange("p (k n) -> p k n", k=2),
                                    in_=w_pe_d.rearrange("(k p) n -> p k n", p=128)))
        S.dma("pool", "scr_w", [], [("scrw", "pg")],
              lambda e: e.dma_start(out=scr_wpg.rearrange("p (k n) -> p k n", k=8),
                                    in_=w_pg_d.rearrange("(k p) n -> p k n", p=128)))
        scrw_tok = ("scr_w", S.cnt["scr_w"])
        for kk in ("out", "pe", "pg"):
            S.lastw[("scrw", kk)] = scrw_tok

        with ExitStack() as s1:
            sb1 = lambda name, shape, dty: s1.enter_context(nc.sbuf_tensor(name, shape, dty))
            tp0 = s1.enter_context(nc.psum_tensor("tpA", [128, 1024], BF16))
            tp = [tp0, tp0]
            ps = [s1.enter_context(nc.psum_tensor("psA%d" % i, [128, 512], F32)) for i in range(7)]
            NB[0] = 7
            identf = sb1("identf", [128, 128], F32)
            dg = sb1("dg", [128, 4, 8, 128], BF16)
            xin = sb1("xin", [128, D], F32)
            g1b = sb1("g1b", [128, D], F32)
            hb = [sb1("hb%d" % i, [128, D], BF16) for i in range(2)]
            hT = [sb1("hT%d" % i, [128, 8, 128], BF16) for i in range(2)]
            xsb = sb1("xsb", [128, 8, 131], BF16)
            xc = [sb1("xc%d" % i, [128, 8, 128], F32) for i in range(2)]
            xcb = [sb1("xcb%d" % i, [128, 8, 128], BF16) for i in range(2)]
            thr = sb1("thr", [128, 8, 128], F32)
            thi = sb1("thi", [128, 8, 128], F32)
            za = sb1("za", [128, 8, 128], F32)
            e2 = sb1("e2", [128, 8, 128], F32)
            thg2 = [sb1("thg%d" % i, [128, 8, 128], F32) for i in range(2)]
            sgb = sb1("sgb", [128, 4, 128], F32)
            xbs = sb1("xbs", [128, 4, 143], F32)
            pb2 = sb1("pb2", [128, 4, 143], F32)
            pb3 = sb1("pb3", [128, 4, 143], F32)
            diffb = sb1("diffb", [128, 4, 128], BF16)
            halo_a = sb1("halo_a", [128, NSEQ, 8, 3], BF16)
            halo_b = sb1("halo_b", [128, NSEQ, 4, 15], F32)
            carry = sb1("carry", [128, NSEQ, 8], F32)
            yaT = [sb1("yaT%d" % i, [128, 8, 128], BF16) for i in range(2)]
            ybT = [sb1("ybT%d" % i, [128, 4, 128], BF16) for i in range(2)]
            tha = sb1("tha", [128, 8, 128], F32)
            thb = sb1("thb", [128, 8, 128], F32)
            mT = [sb1("mT%d" % i, [128, 8, 128], BF16) for i in range(2)]

            flat = lambda t: t[:].rearrange("p c t -> p (c t)")

            S.dma("sp", "g1b", [], [("g1b",)], lambda e: e.dma_start(out=g1b[:], in_=norm_g_d.partition_broadcast(128)))
            S.op("pool", [], [("identf",)], lambda e: e.memset(identf[:], 0.0))
            S.op("pool", [("identf",)], [("identf",)],
                 lambda e: e.affine_select(out=identf[:], in_=identf[:], pattern=[[-1, 128]],
                                           compare_op=ALU.not_equal, fill=1.0, base=0, channel_multiplier=1))
            S.op("dve", [("identf",)], [("ident",)], lambda e: e.tensor_copy(out=ident[:], in_=identf[:]))
            def late_setup():
                const_setup(e2, za, identf, ps)
                for tap in range(4):
                    for c in range(8):
                        S.op("dve", [("identf",), ("cst", "cw")], [("dg",)],
                             lambda e, tap=tap, c=c: e.tensor_scalar(out=dg[:, tap, c, :], in0=identf[:],
                                                                     scalar1=col(CW + tap * 8 + c), scalar2=None,
                                                                     op0=ALU.mult))
                S.op("pool", [], [("pb2",)], lambda e: e.memset(pb2[:], 0.0))
                S.op("pool", [], [("pb3",)], lambda e: e.memset(pb3[:], 0.0))
                S.op("pool", [], [("halo_a", s) for s in range(NSEQ)], lambda e: e.memset(halo_a[:], 0.0))
                S.op("pool", [], [("halo_b", s) for s in range(NSEQ)], lambda e: e.memset(halo_b[:], 0.0))
                S.op("pool", [], [("carry", s, c) for s in range(NSEQ) for c in range(8)],
                     lambda e: e.memset(carry[:], 0.0))


            def rows(j):
                seq, tt = j % NSEQ, j // NSEQ
                return seq, tt * T

            def rstd_ops(which, j, src_key):
                ssc = stats[:, which, j:j + 1]
                rsc = stats[:, which + 1, j:j + 1]
                S.op("pool", [src_key], [("rs", which, j)],
                     lambda e: e.tensor_scalar(out=rsc, in0=ssc, scalar1=1.0 / D, scalar2=EPS, op0=ALU.mult,
                                               op1=ALU.add))
                S.op("pool", [("rs", which, j), ("cst", "mh")], [("rs", which, j)],
                     lambda e: e.tensor_tensor(out=rsc, in0=rsc, in1=col(MH), op=ALU.pow))
                return rsc

            def a_elem(j):
                seq, r0 = rows(j)
                b = j % 2
                S.dma("sp", "xin", [], [("xin",)], lambda e: e.dma_start(out=xin[:], in_=x_d[seq, r0:r0 + T, :]))
                S.op("act", [("xin",)], [("hb", b), ("ss", 0, j)],
                     lambda e: e.activation(out=hb[b][:], in_=xin[:], func=AF.Square,
                                            accum_out=stats[:, 0, j:j + 1]))
                rsc = rstd_ops(0, j, ("ss", 0, j))
                S.op("dve", [("xin",), ("rs", 0, j), ("g1b",)], [("hb", b)],
                     lambda e: e.scalar_tensor_tensor(out=hb[b][:], in0=xin[:], scalar=rsc, in1=g1b[:], op0=ALU.mult,
                                                      op1=ALU.mult))

            def a_tr(j):
                b = j % 2

                def tr(e):
                    ins = None
                    for k in range(8):
                        ins = e.transpose(tp[b][:, k * 128:(k + 1) * 128], hb[b][:, k * 128:(k + 1) * 128], ident[:])
                    return ins

                S.op("pe", [("hb", b), ("ident",)], [("tp", 0)], tr)
                S.op("act", [("tp", 0)], [("hT", b)],
                     lambda e: e.activation(out=flat(hT[b]), in_=tp[b][:], func=AF.Copy))

            def win_quad(j, col0, bk, n=4):
                b = j % 2

                def f(e):
                    ins = None
                    for q in range(n):
                        for k in range(8):
                            ins = e.matmul(ps[bk][:, q * 128:(q + 1) * 128],
                                           lhsT=w_in[:, k, col0 + q * 128:col0 + (q + 1) * 128],
                                           rhs=hT[b][:, k, :], start=(k == 0), stop=(k == 7))
                    return ins
                return f

            def f_xa(j):
                seq, _ = rows(j)
                b = j % 2
                for Q in range(2):
                    bk = nbank()
                    S.op("pe", [("w", "in", Q), ("hT", b)], [("ps", bk)], win_quad(j, O_XA + Q * 512, bk))
                    S.op("act", [("ps", bk)], [("xsb", Q)],
                         lambda e, bk=bk, Q=Q: e.activation(
                             out=xsb[:, 4 * Q:4 * Q + 4, 3:131],
                             in_=ps[bk][:].rearrange("p (c t) -> p c t", c=4), func=AF.Copy))
                S.op("pool", [("halo_a", seq)], [("xsb_h",)],
                     lambda e: e.tensor_copy(out=xsb[:, :, 0:3], in_=halo_a[:, seq, :, :]))
                S.op("pool", [("xsb", 0), ("xsb", 1)], [("halo_a", seq)],
                     lambda e: e.tensor_copy(out=halo_a[:, seq, :, :], in_=xsb[:, :, 128:131]))

            def f_xb(j):
                seq, r0 = rows(j)
                b = j % 2
                first = (r0 == 0)
                bk = nbank()
                S.op("pe", [("w", "in", 4), ("hT", b)], [("ps", bk)], win_quad(j, O_XB, bk))
                S.op("act", [("ps", bk)], [("xbs",)],
                     lambda e: e.activation(out=xbs[:, :, 15:143], in_=ps[bk][:].rearrange("p (c t) -> p c t", c=4),
                                            func=AF.Copy))
                S.op("pool", [("halo_b", seq)], [("xbs_h",)],
                     lambda e: e.tensor_copy(out=xbs[:, :, 0:15], in_=halo_b[:, seq, :, :]))
                S.op("dve", [("xbs",), ("xbs_h",)], [("pb2",)],
                     lambda e: e.tensor_tensor(out=pb2[:, 0:4, 1:143], in0=xbs[:, 0:4, 1:143], in1=xbs[:, 0:4, 0:142],
                                               op=ALU.add))
                S.op("dve", [("pb2",)], [("pb3",)],
                     lambda e: e.tensor_tensor(out=pb3[:, 1:4, 2:143], in0=pb2[:, 1:4, 2:143], in1=pb2[:, 1:4, 0:141],
                                               op=ALU.add))
                S.op("dve", [("pb3",), ("pb2",)], [("pb2",)],
                     lambda e: e.tensor_tensor(out=pb2[:, 2:4, 4:143], in0=pb3[:, 2:4, 4:143], in1=pb3[:, 2:4, 0:139],
                                               op=ALU.add))
                S.op("dve", [("pb2",), ("pb3",)], [("pb3",)],
                     lambda e: e.tensor_tensor(out=pb3[:, 3, 8:143], in0=pb2[:, 3, 8:143], in1=pb2[:, 3, 0:135],
                                               op=ALU.add))
                for g, kw in enumerate(WINS):
                    src = pb2 if g % 2 == 0 else pb3
                    if first:
                        S.op("dve", [("pb2",), ("pb3",), ("rk",)], [("pb2",), ("pb3",)],
                             lambda e, src=src, g=g: e.tensor_tensor(out=src[:, g, 15:30], in0=src[:, g, 15:30],
                                                                     in1=rk[:, g, :], op=ALU.mult))
                        S.op("dve", [("pb2",), ("pb3",), ("xbs",)], [("diffb", g)],
                             lambda e, src=src, g=g, kw=kw: e.scalar_tensor_tensor(
                                 out=diffb[:, g, 15:128], in0=src[:, g, 30:143], scalar=1.0 / kw,
                                 in1=xbs[:, g, 30:143], op0=ALU.mult, op1=ALU.subtract))
                        S.op("dve", [("pb2",), ("pb3",), ("xbs",)], [("diffb", g)],
                             lambda e, src=src, g=g: e.tensor_tensor(out=diffb[:, g, 0:15], in0=src[:, g, 15:30],
                                                                     in1=xbs[:, g, 15:30], op=ALU.subtract))
                    else:
                        S.op("dve", [("pb2",), ("pb3",), ("xbs",)], [("diffb", g)],
                             lambda e, src=src, g=g, kw=kw: e.scalar_tensor_tensor(
                                 out=diffb[:, g, :], in0=src[:, g, 15:143], scalar=1.0 / kw, in1=xbs[:, g, 15:143],
                                 op0=ALU.mult, op1=ALU.subtract))
                S.op("pool", [("xbs",)], [("halo_b", seq)],
                     lambda e: e.tensor_copy(out=halo_b[:, seq, :, :], in_=xbs[:, :, 128:143]))

            def f_gates_in(j):
                b = j % 2
                thg = thg2[j % 2]
                for Q in range(2):
                    bk = nbank()
                    S.op("pe", [("w", "in", 2 + Q), ("hT", b)], [("ps", bk)], win_quad(j, O_GA + Q * 512, bk))
                    S.op("act", [("ps", bk)], [("thg", j % 2, Q)],
                         lambda e, bk=bk, Q=Q: e.activation(
                             out=thg[:, 4 * Q:4 * Q + 4, :].rearrange("p c t -> p (c t)"), in_=ps[bk][:], func=AF.Tanh,
                             scale=0.5))
                    S.op("dve", [("thg", j % 2, Q), ("ps", bk)], [("thg", j % 2, Q)],
                         lambda e, bk=bk, Q=Q: e.scalar_tensor_tensor(
                             out=thg[:, 4 * Q:4 * Q + 4, :].rearrange("p c t -> p (c t)"),
                             in0=thg[:, 4 * Q:4 * Q + 4, :].rearrange("p c t -> p (c t)"), scalar=1.0, in1=ps[bk][:],
                             op0=ALU.add, op1=ALU.mult))
                bk = nbank()
                S.op("pe", [("w", "in", 5), ("hT", b)], [("ps", bk)], win_quad(j, O_GB, bk))
                S.op("act", [("ps", bk)], [("sgb",)],
                     lambda e, bk=bk: e.activation(out=flat(sgb), in_=ps[bk][:], func=AF.Tanh, scale=0.5))
                S.op("dve", [("sgb",), ("ps", bk)], [("sgb",)],
                     lambda e, bk=bk: e.scalar_tensor_tensor(out=flat(sgb), in0=flat(sgb), scalar=1.0, in1=ps[bk][:],
                                                             op0=ALU.add, op1=ALU.mult))

            def f_conv(j):
                jj = j % 2
                for Q in range(2):
                    bk = nbank()

                    def f(e, bk=bk, Q=Q):
                        ins = None
                        for q in range(4):
                            c = 4 * Q + q
                            for tap in range(4):
                                ins = e.matmul(ps[bk][:, q * 128:(q + 1) * 128], lhsT=dg[:, tap, c, :],
                                               rhs=xsb[:, c, tap:tap + 128], start=(tap == 0), stop=(tap == 3))
                        return ins

                    S.op("pe", [("dg",), ("xsb", 0), ("xsb", 1), ("xsb_h",)], [("ps", bk)], f)
                    for q in range(4):
                        c = 4 * Q + q
                        S.op("act", [("ps", bk), ("cst", "cb")], [("xc", jj, c)],
                             lambda e, bk=bk, q=q, c=c: e.activation(out=xc[jj][:, c, :],
                                                                     in_=ps[bk][:, q * 128:(q + 1) * 128],
                                                                     func=AF.Identity, bias=col(CB + c)))
                S.op("act", [("xc", jj, c) for c in range(8)], [("xcb", jj)],
                     lambda e: e.activation(out=flat(xcb[jj]), in_=flat(xc[jj]), func=AF.Copy))

            def f_lru(j):
                seq, _ = rows(j)
                jj, b = j % 2, j % 2
                XC = [("xc", jj, c) for c in range(8)]
                for Q in range(2):
                    br, bi = nbank(), nbank()

                    def f(e, br=br, bi=bi, Q=Q):
                        ins = None
                        for q in range(4):
                            c = 4 * Q + q
                            e.matmul(ps[br][:, q * 128:(q + 1) * 128], lhsT=w_a[:, c, :], rhs=xcb[jj][:, c, :],
                                     start=True, stop=True)
                        for q in range(4):
                            c = 4 * Q + q
                            ins = e.matmul(ps[bi][:, q * 128:(q + 1) * 128], lhsT=w_x[:, c, :], rhs=xcb[jj][:, c, :],
                                           start=True, stop=True)
                        return ins

                    S.op("pe", [("w", "a"), ("w", "x"), ("xcb", jj)], [("ps", br), ("ps", bi)], f)
                    for q in range(4):
                        c = 4 * Q + q
                        S.op("act", [("ps", br), ("cst", "hba")], [("thr", c)],
                             lambda e, br=br, q=q, c=c: e.activation(out=thr[:, c, :],
                                                                     in_=ps[br][:, q * 128:(q + 1) * 128],
                                                                     func=AF.Tanh, scale=0.5, bias=col(HBA + c)))
                    for q in range(4):
                        c = 4 * Q + q
                        S.op("act", [("ps", bi), ("cst", "hba")], [("thi", c)],
                             lambda e, bi=bi, q=q, c=c: e.activation(out=thi[:, c, :],
                                                                     in_=ps[bi][:, q * 128:(q + 1) * 128],
                                                                     func=AF.Tanh, scale=0.5, bias=col(HBX + c)))
                THR = [("thr", c) for c in range(8)]
                S.op("dve", THR + [("cst", "ch")], [("za",)],
                     lambda e: e.scalar_tensor_tensor(out=za[:], in0=thr[:], scalar=1.0,
                                                      in1=cst[:, CH:CH + 8].unsqueeze(2).broadcast_to([128, 8, 128]),
                                                      op0=ALU.add, op1=ALU.mult))

            def f_lru2(j):
                jj = j % 2
                XC = [("xc", jj, c) for c in range(8)]
                THI = [("thi", c) for c in range(8)]
                S.op("act", [("za",)], [("e2",)] + [("h", c) for c in range(8)],
                     lambda e: e.activation(out=flat(e2), in_=flat(za), func=AF.Exp, scale=2.0))
                S.op("pool", [("e2",)], [("e2",)],
                     lambda e: e.tensor_scalar(out=flat(e2), in0=flat(e2), scalar1=1.0, scalar2=0.0, op0=ALU.min,
                                               op1=ALU.max))
                S.op("act", [("za",)], [("za",)],
                     lambda e: e.activation(out=flat(za), in_=flat(za), func=AF.Exp))
                S.op("dve", THI + XC, XC,
                     lambda e: e.scalar_tensor_tensor(out=flat(xc[jj]), in0=flat(thi), scalar=1.0, in1=flat(xc[jj]),
                                                      op0=ALU.add, op1=ALU.mult))

            def f_lru3(j):
                seq, _ = rows(j)
                jj, b = j % 2, j % 2
                XC = [("xc", jj, c) for c in range(8)]
                S.op("act", [("e2",)], [("e2",)],
                     lambda e: e.activation(out=flat(e2), in_=flat(e2), func=AF.Sqrt, scale=-1.0, bias=1.0))

            def f_lru3b(j):
                seq, _ = rows(j)
                jj, b = j % 2, j % 2
                XC = [("xc", jj, c) for c in range(8)]
                S.op("dve", [("e2",)] + XC, XC,
                     lambda e: e.scalar_tensor_tensor(out=flat(xc[jj]), in0=flat(xc[jj]), scalar=0.5, in1=flat(e2),
                                                      op0=ALU.mult, op1=ALU.mult))
                for c in range(8):
                    S.op("dve", [("za",), ("xc", jj, c), ("carry", seq, c), ("e2",)], [("h", c)],
                         lambda e, c=c: e.tensor_tensor_scan(out=e2[:, c, :], data0=za[:, c, :], data1=xc[jj][:, c, :],
                                                             initial=carry[:, seq, c:c + 1], op0=ALU.mult,
                                                             op1=ALU.add))
                    S.op("pool", [("h", c)], [("carry", seq, c)],
                         lambda e, c=c: e.tensor_copy(out=carry[:, seq, c:c + 1], in_=e2[:, c, 127:128]))
                H = [("h", c) for c in range(8)]
                thg = thg2[j % 2]
                S.op("dve", H + [("thg", j % 2, 0), ("thg", j % 2, 1)], [("yaT", b), ("e2",)],
                     lambda e: e.scalar_tensor_tensor(out=flat(yaT[b]), in0=flat(e2), scalar=0.5, in1=flat(thg),
                                                      op0=ALU.mult, op1=ALU.mult))

            def f_poolw(j):
                b = j % 2
                bk = nbank()

                def f(e):
                    ins = None
                    for g in range(4):
                        ins = e.matmul(ps[bk][:, g * 128:(g + 1) * 128], lhsT=pw[:, g, :], rhs=diffb[:, g, :],
                                       start=True, stop=True)
                    return ins

                S.op("pe", [("w", "pw")] + [("diffb", g) for g in range(4)], [("ps", bk)], f)
                for g in range(4):
                    S.op("dve", [("ps", bk), ("sgb",), ("cst", "hpsc")], [("ybT", b)],
                         lambda e, g=g: e.scalar_tensor_tensor(out=ybT[b][:, g, :], in0=ps[bk][:, g * 128:(g + 1) * 128],
                                                               scalar=col(HPSC + g), in1=sgb[:, g, :], op0=ALU.mult,
                                                               op1=ALU.mult))

            def m_gate(j):
                b = j % 2
                for Q in range(2):
                    for which, off, dst in (("tha", O_MA, tha), ("thb", O_MB, thb)):
                        bk = nbank()
                        S.op("pe", [("w", "in", (off + Q * 512) // 512), ("hT", b)], [("ps", bk)], win_quad(j, off + Q * 512, bk))
                        S.op("act", [("ps", bk)], [(which, Q)],
                             lambda e, bk=bk, Q=Q, dst=dst: e.activation(
                                 out=dst[:, 4 * Q:4 * Q + 4, :].rearrange("p c t -> p (c t)"), in_=ps[bk][:],
                                 func=AF.Tanh, scale=0.5))

            def m_proj(j):
                b = j % 2
                for Q in range(2):
                    ba_, bb_ = nbank(), nbank()

                    def fa(e, ba_=ba_, Q=Q):
                        ins = None
                        for q in range(4):
                            o = 4 * Q + q
                            for k in range(8):
                                ins = e.matmul(ps[ba_][:, q * 128:(q + 1) * 128], lhsT=w_pl[:, k, o * 128:(o + 1) * 128],
                                               rhs=yaT[b][:, k, :], start=(k == 0), stop=(k == 7))
                        return ins

                    def fb(e, bb_=bb_, Q=Q):
                        ins = None
                        for q in range(4):
                            o = 4 * Q + q
                            for k in range(4):
                                ins = e.matmul(ps[bb_][:, q * 128:(q + 1) * 128], lhsT=w_pp[:, k, o * 128:(o + 1) * 128],
                                               rhs=ybT[b][:, k, :], start=(k == 0), stop=(k == 3))
                        return ins

                    S.op("pe", [("w", "pl"), ("yaT", b)], [("ps", ba_)], fa)
                    S.op("pe", [("w", "pp"), ("ybT", b)], [("ps", bb_)], fb)
                    sl = lambda t, Q=Q: t[:, 4 * Q:4 * Q + 4, :].rearrange("p c t -> p (c t)")
                    S.op("dve", [("tha", Q), ("ps", ba_)], [("tha", Q)],
                         lambda e, ba_=ba_, sl=sl: e.scalar_tensor_tensor(out=sl(tha), in0=sl(tha), scalar=1.0,
                                                                          in1=ps[ba_][:], op0=ALU.add, op1=ALU.mult))
                    S.op("dve", [("thb", Q), ("ps", bb_)], [("thb", Q)],
                         lambda e, bb_=bb_, sl=sl: e.scalar_tensor_tensor(out=sl(thb), in0=sl(thb), scalar=1.0,
                                                                          in1=ps[bb_][:], op0=ALU.add, op1=ALU.mult))
                    S.op("dve", [("tha", Q), ("thb", Q)], [("mT", b, Q)],
                         lambda e, sl=sl: e.tensor_tensor(out=sl(mT[b]), in0=sl(tha), in1=sl(thb), op=ALU.add))
                S.dma("sp", "mst%d" % b, [("mT", b, 0), ("mT", b, 1)], [("scr", j)],
                      lambda e: e.dma_start(out=m_scr[j], in_=flat(mT[b])))

            a_elem(0)
            a_tr(0)
            late_setup()
            f_xb(0)
            for j in range(NT + 1):
                if j + 1 < NT:
                    a_elem(j + 1)
                if j < NT:
                    f_xa(j)
                if j >= 1:
                    f_lru3(j - 1)
                if j < NT:
                    f_gates_in(j)
                if j >= 1:
                    f_lru3b(j - 1)
                if j < NT:
                    f_conv(j)
                if j >= 1:
                    m_gate(j - 1)
                if j < NT:
                    f_lru(j)
                if j + 1 < NT:
                    a_tr(j + 1)
                if j < NT:
                    f_lru2(j)
                if j >= 1:
                    m_proj(j - 1)
                if j < NT:
                    f_poolw(j)
                if j + 1 < NT:
                    f_xb(j + 1)
            S.barrier()
            S.flush()

        with ExitStack() as s2:
            sb2 = lambda name, shape, dty: s2.enter_context(nc.sbuf_tensor(name, shape, dty))
            tp = [s2.enter_context(nc.psum_tensor("tpB%d" % i, [128, 1024], BF16)) for i in range(2)]
            ps = [s2.enter_context(nc.psum_tensor("psB%d" % i, [128, 512], F32)) for i in range(6)]
            NB[0] = 6
            bank_i[0] = 0
            w_out = sb2("w_out_sb", [128, 8, D], BF16)
            w_pg = sb2("w_pg_sb", [128, 8, D], BF16)
            w_pe = sb2("w_pe_sb", [128, 2, D], BF16)
            def load_wout(h):
                S.dma("sp", "w2_out%d" % h, [("scrw", "out")], [("w", "out", h)],
                      lambda e: e.dma_start(out=w_out[:, :, h * 512:(h + 1) * 512],
                                            in_=scr_wout.rearrange("p (k n) -> p k n", k=8)[:, :, h * 512:(h + 1) * 512]))

            fg = sb2("fg", [128, D], F32)
            g2b = sb2("g2b", [128, D], F32)
            xr = [sb2("xr%d" % i, [128, D], F32) for i in range(5)]
            mIn = [sb2("mIn%d" % i, [128, 8, 128], BF16) for i in range(3)]
            x1n = [sb2("x1n%d" % i, [128, D], BF16) for i in range(3)]
            junk = sb2("junk", [128, D], BF16)
            x1nT = [sb2("x1nT%d" % i, [128, 8, 128], BF16) for i in range(3)]
            th = [sb2("th%d" % i, [128, 512], F32) for i in range(2)]
            pin = [sb2("pin%d" % i, [128, PD], F32) for i in range(3)]
            pbf = [sb2("pbf%d" % i, [128, PD], BF16) for i in range(3)]
            pT = [sb2("pT%d" % i, [128, 2, 128], BF16) for i in range(3)]


            def rows(j):
                seq, tt = j % NSEQ, j // NSEQ
                return seq, tt * T

            def rstd2(which, j, src_key):
                ssc = stats[:, which, j:j + 1]
                rsc = stats[:, which + 1, j:j + 1]
                S.op("pool", [src_key], [("rs", which, j)],
                     lambda e: e.tensor_scalar(out=rsc, in0=ssc, scalar1=1.0 / D, scalar2=EPS, op0=ALU.mult,
                                               op1=ALU.add))
                S.op("pool", [("rs", which, j), ("cst", "mh")], [("rs", which, j)],
                     lambda e: e.tensor_tensor(out=rsc, in0=rsc, in1=col(MH), op=ALU.pow))
                return rsc

            def load(j):
                seq, r0 = rows(j)
                a, b = j % 5, j % 3
                S.dma("sp", "xr%d" % a, [], [("xr", a, 0), ("xr", a, 1)],
                      lambda e: e.dma_start(out=xr[a][:], in_=x_d[seq, r0:r0 + T, :]))
                S.dma("sp", "mIn%d" % b, [("scr", j)], [("mIn", b)],
                      lambda e: e.dma_start(out=mIn[b][:].rearrange("p k t -> p (k t)"), in_=m_scr[j]))
                S.dma("sp", "pin%d" % b, [], [("pin", b)],
                      lambda e: e.dma_start(out=pin[b][:], in_=p_d[seq, r0:r0 + T, :]))

            def stage_d(j):
                a, b = j % 5, j % 3
                for h in range(2):
                    bk = nbank()

                    def f(e, bk=bk, h=h):
                        ins = None
                        for k in range(8):
                            ins = e.matmul(ps[bk][:], lhsT=mIn[b][:, k, :], rhs=w_out[:, k, h * 512:(h + 1) * 512],
                                           start=(k == 0), stop=(k == 7))
                        return ins

                    S.op("pe", [("w", "out", h), ("mIn", b)], [("ps", bk)], f)
                    S.op("dve", [("ps", bk), ("xr", a, h)], [("xr", a, h)],
                         lambda e, bk=bk, h=h: e.scalar_tensor_tensor(out=xr[a][:, h * 512:(h + 1) * 512], in0=ps[bk][:],
                                                                      scalar=0.5, in1=xr[a][:, h * 512:(h + 1) * 512],
                                                                      op0=ALU.mult, op1=ALU.add))

            def stage_d2(j):
                a, b = j % 5, j % 3
                S.op("act", [("xr", a, 0), ("xr", a, 1)], [("x1n", b), ("ss", 2, j)],
                     lambda e: e.activation(out=x1n[b][:], in_=xr[a][:], func=AF.Square, accum_out=stats[:, 2, j:j + 1]))
                rsc = rstd2(2, j, ("ss", 2, j))
                S.op("dve", [("xr", a, 0), ("xr", a, 1), ("rs", 2, j), ("g2b",)], [("x1n", b)],
                     lambda e: e.scalar_tensor_tensor(out=x1n[b][:], in0=xr[a][:], scalar=rsc, in1=g2b[:], op0=ALU.mult,
                                                      op1=ALU.mult))
                S.op("dve", [("pin", b)], [("pbf", b)], lambda e: e.tensor_copy(out=pbf[b][:], in_=pin[b][:]))

            def stage_t(j):
                b = j % 3

                def tr(e):
                    ins = None
                    for k in range(8):
                        ins = e.transpose(tp[0][:, k * 128:(k + 1) * 128], x1n[b][:, k * 128:(k + 1) * 128], ident[:])
                    return ins

                S.op("pe", [("x1n", b), ("ident",)], [("tp", 0)], tr)
                S.op("act", [("tp", 0)], [("x1nT", b)],
                     lambda e: e.activation(out=x1nT[b][:].rearrange("p k t -> p (k t)"), in_=tp[0][:], func=AF.Copy))

                def trp(e):
                    e.transpose(tp[1][:, 0:128], pbf[b][:, 0:128], ident[:])
                    return e.transpose(tp[1][:, 128:256], pbf[b][:, 128:256], ident[:])

                S.op("pe", [("pbf", b), ("ident",)], [("tp", 1)], trp)
                S.op("act", [("tp", 1)], [("pT", b)],
                     lambda e: e.activation(out=pT[b][:].rearrange("p k t -> p (k t)"), in_=tp[1][:, 0:256], func=AF.Copy))

            def stage_e(j):
                seq, r0 = rows(j)
                a, b = j % 5, j % 3
                for h in range(2):
                    bg = nbank()

                    def fgate(e, bg=bg, h=h):
                        ins = None
                        for k in range(8):
                            ins = e.matmul(ps[bg][:], lhsT=x1nT[b][:, k, :], rhs=w_pg[:, k, h * 512:(h + 1) * 512],
                                           start=(k == 0), stop=(k == 7))
                        return ins

                    S.op("pe", [("w", "pg"), ("x1nT", b)], [("ps", bg)], fgate)
                    S.op("act", [("ps", bg)], [("th", h)],
                         lambda e, bg=bg, h=h: e.activation(out=th[h][:], in_=ps[bg][:], func=AF.Tanh, scale=0.5))
                    be = nbank()

                    def fpe(e, be=be, h=h):
                        e.matmul(ps[be][:], lhsT=pT[b][:, 0, :], rhs=w_pe[:, 0, h * 512:(h + 1) * 512], start=True,
                                 stop=False)
                        return e.matmul(ps[be][:], lhsT=pT[b][:, 1, :], rhs=w_pe[:, 1, h * 512:(h + 1) * 512],
                                        start=False, stop=True)

                    S.op("pe", [("w", "pe"), ("pT", b)], [("ps", be)], fpe)
                    S.op("dve", [("th", h), ("ps", be)], [("th", h)],
                         lambda e, be=be, h=h: e.scalar_tensor_tensor(out=th[h][:], in0=th[h][:], scalar=1.0,
                                                                      in1=ps[be][:], op0=ALU.add, op1=ALU.mult))
                    S.op("dve", [("th", h), ("xr", a, h)], [("xr", a, h)],
                         lambda e, h=h: e.scalar_tensor_tensor(out=xr[a][:, h * 512:(h + 1) * 512], in0=th[h][:],
                                                               scalar=0.5, in1=xr[a][:, h * 512:(h + 1) * 512],
                                                               op0=ALU.mult, op1=ALU.add))

            def stage_e2(j):
                seq, r0 = rows(j)
                a, b = j % 5, j % 3
                S.op("act", [("xr", a, 0), ("xr", a, 1)], [("junk",), ("ss", 4, j)],
                     lambda e: e.activation(out=junk[:], in_=xr[a][:], func=AF.Square, accum_out=stats[:, 4, j:j + 1]))
                rsc = rstd2(4, j, ("ss", 4, j))
                S.op("dve", [("xr", a, 0), ("xr", a, 1), ("rs", 4, j), ("fg",)], [("xr", a, 0), ("xr", a, 1)],
                     lambda e: e.scalar_tensor_tensor(out=xr[a][:], in0=xr[a][:], scalar=rsc, in1=fg[:], op0=ALU.mult,
                                                      op1=ALU.mult))
                S.dma("sp", "yst%d" % a, [("xr", a, 0), ("xr", a, 1)], [("y", j)],
                      lambda e: e.dma_start(out=y_d[seq, r0:r0 + T, :], in_=xr[a][:]))

            S.dma("sp", "mIn0", [("scr", 0)], [("mIn", 0)],
                  lambda e: e.dma_start(out=mIn[0][:].rearrange("p k t -> p (k t)"), in_=m_scr[0]))
            load_wout(0)
            S.dma("sp", "xr0", [], [("xr", 0, 0), ("xr", 0, 1)],
                  lambda e: e.dma_start(out=xr[0][:], in_=x_d[0, 0:T, :]))
            S.dma("sp", "pin0", [], [("pin", 0)], lambda e: e.dma_start(out=pin[0][:], in_=p_d[0, 0:T, :]))
            load_wout(1)
            S.dma("sp", "g2b", [], [("g2b",)], lambda e: e.dma_start(out=g2b[:], in_=ple_g_d.partition_broadcast(128)))
            load(1)
            S.dma("sp", "w2_pe", [("scrw", "pe")], [("w", "pe")],
                  lambda e: e.dma_start(out=w_pe[:].rearrange("p k n -> p (k n)"), in_=scr_wpe))
            S.dma("sp", "w2_pg", [("scrw", "pg")], [("w", "pg")],
                  lambda e: e.dma_start(out=w_pg[:].rearrange("p k n -> p (k n)"), in_=scr_wpg))
            S.dma("sp", "fg", [], [("fg",)], lambda e: e.dma_start(out=fg[:], in_=fg_d.partition_broadcast(128)))
            for j in range(NT + 2):
                if j + 2 < NT:
                    load(j + 2)
                if j < NT:
                    stage_d(j)
                if j >= 2:
                    stage_e(j - 2)
                if 1 <= j <= NT:
                    stage_t(j - 1)
                if j < NT:
                    stage_d2(j)
                if j >= 2:
                    stage_e2(j - 2)
            S.barrier()
            S.flush()
    return nc


_CACHE = {}


def kernel(**inputs):
    f32 = lambda a: np.ascontiguousarray(np.asarray(a, dtype=np.float32))
    x = f32(inputs["x"])
    p = f32(inputs["p"])[0]
    shared = {
        "norm_g": f32(inputs["norm_g"])[0], "w_in": f32(inputs["w_in"])[0],
        "conv_w": f32(inputs["conv_w"])[0], "conv_b": f32(inputs["conv_b"])[0],
        "lru_w_a": f32(inputs["lru_w_a"])[0], "lru_b_a": f32(inputs["lru_b_a"])[0].reshape(-1),
        "lru_w_x": f32(inputs["lru_w_x"])[0], "lru_b_x": f32(inputs["lru_b_x"])[0].reshape(-1),
        "lru_lambda": f32(inputs["lru_lambda"])[0], "pool_w": f32(inputs["pool_w"])[0],
        "pool_scale": f32(inputs["pool_scale"])[0], "w_proj_lru": f32(inputs["w_proj_lru"])[0],
        "w_proj_pool": f32(inputs["w_proj_pool"])[0], "w_out": f32(inputs["w_out"])[0],
        "ple_norm_g": f32(inputs["ple_norm_g"])[0], "w_ple_gate": f32(inputs["w_ple_gate"])[0],
        "w_ple_proj": f32(inputs["w_ple_proj"])[0], "final_g": f32(inputs["final_g"]),
    }
    shared = {k: np.ascontiguousarray(v) for k, v in shared.items()}
    if "nc" not in _CACHE:
        _CACHE["nc"] = build_program()
    nc = _CACHE["nc"]
    in_maps = []
    for c in range(NCORES):
        m = dict(shared)
        m["x"] = np.ascontiguousarray(x[c * NSEQ:(c + 1) * NSEQ])
        m["p"] = np.ascontiguousarray(p[c * NSEQ:(c + 1) * NSEQ])
        in_maps.append(m)
    res = run_bass_kernel_spmd(nc, in_maps, core_ids=list(range(NCORES)))
    out = np.concatenate([np.asarray(r["y"]) for r in res.results], axis=0)
    return out.astype(np.float32, copy=False)
```
